# Optimizing a Trainium2 kernel written in Bass

```python
import math
import jax, jax.numpy as jnp
from jax import lax
import numpy as np

D_MODEL = 2048
BATCH = 4
SEQ = 4096
DEPTH = 2

CHUNK = 64
Q_BLOCK = 128
HEAD_DIM = 128
EPS = 1e-6
A_HEADS = 6
A_DIM = A_HEADS * HEAD_DIM
B_HEADS = 6
B_Q_LORA = 512
B_KV_LORA = 256
B_NOPE = 128
B_ROPE = 64
B_V = 128
B_DIM = B_HEADS * B_V
ROPE_THETA = 10000.0
C_HEADS = 4
C_QK = 64
C_V = 2 * C_QK
C_DIM = C_HEADS * C_V
REL_BUCKETS = 32
REL_MAX_DIST = 128
D_MIX = A_DIM + B_DIM + C_DIM
IN_SIZES = (A_DIM, A_DIM, A_DIM, A_HEADS, A_DIM,
            B_Q_LORA, B_KV_LORA, B_ROPE, B_DIM,
            2 * C_HEADS * C_QK, 2 * C_HEADS * C_QK, C_DIM, C_DIM)
IN_OFFSETS = tuple(int(v) for v in np.cumsum(IN_SIZES)[:-1])
N_IN = int(sum(IN_SIZES))

kernel_name = "hybrid_fox_mla_diff_stream_encoder"


def _rms_norm(x, g):
    xf = x.astype(jnp.float32)
    y = xf * lax.rsqrt(jnp.mean(xf * xf, axis=-1, keepdims=True) + EPS)
    return (y * g.astype(jnp.float32)).astype(x.dtype)


def _rope(x, pos):
    half = x.shape[-1] // 2
    inv = ROPE_THETA ** (-jnp.arange(half, dtype=jnp.float32) / half)
    ang = pos.astype(jnp.float32)[:, None] * inv[None, :]
    cos = jnp.cos(ang)[None, :, None, :].astype(x.dtype)
    sin = jnp.sin(ang)[None, :, None, :].astype(x.dtype)
    x1, x2 = x[..., :half], x[..., half:]
    return jnp.concatenate([x1 * cos - x2 * sin, x1 * sin + x2 * cos], axis=-1)


def _sweep(block_fn, seq):
    return jnp.concatenate([block_fn(i * Q_BLOCK, (i + 1) * Q_BLOCK)
                            for i in range(seq // Q_BLOCK)], axis=1)


def _chunk_mask(q0, q1):
    tq = jnp.arange(q0, q1) // CHUNK
    tk = jnp.arange(q1) // CHUNK
    return tk[None, :] <= tq[:, None]


def _frame_mask(q0, q1):
    return jnp.arange(q1)[None, :] <= jnp.arange(q0, q1)[:, None]


def _t5_bucket(rel):
    nb = REL_BUCKETS // 2
    max_exact = nb // 2
    ret = (rel > 0).astype(jnp.int32) * nb
    n = jnp.abs(rel)
    nf = jnp.maximum(n, 1).astype(jnp.float32)
    large = max_exact + (jnp.log(nf / max_exact) / math.log(REL_MAX_DIST / max_exact)
                         * (nb - max_exact)).astype(jnp.int32)
    large = jnp.minimum(large, nb - 1)
    return ret + jnp.where(n < max_exact, n, large)


def _fox_attention(q, k, v, cum_logf):
    scale = 1.0 / math.sqrt(q.shape[-1])
    cum_t = jnp.transpose(cum_logf, (0, 2, 1))

    def blk(q0, q1):
        s = jnp.einsum('bqhd,bkhd->bhqk', q[:, q0:q1], k[:, :q1]).astype(jnp.float32) * scale
        decay = cum_t[:, :, q0:q1, None] - cum_t[:, :, None, :q1]
        s = jnp.where(_frame_mask(q0, q1), s + decay, -jnp.inf)
        p = jax.nn.softmax(s, axis=-1).astype(v.dtype)
        return jnp.einsum('bhqk,bkhd->bqhd', p, v[:, :q1])

    return _sweep(blk, q.shape[1])


def _mla_attention(q_nope, q_rope, k_nope, k_rope, v):
    scale = 1.0 / math.sqrt(B_NOPE + B_ROPE)

    def blk(q0, q1):
        s = (jnp.einsum('bqhd,bkhd->bhqk', q_nope[:, q0:q1], k_nope[:, :q1])
             + jnp.einsum('bqhr,bkr->bhqk', q_rope[:, q0:q1], k_rope[:, :q1]))
        s = jnp.where(_chunk_mask(q0, q1), s.astype(jnp.float32) * scale, -jnp.inf)
        p = jax.nn.softmax(s, axis=-1).astype(v.dtype)
        return jnp.einsum('bhqk,bkhd->bqhd', p, v[:, :q1])

    return _sweep(blk, q_nope.shape[1])


def _diff_attention(q1, q2, k1, k2, v, lam, rel_bias):
    scale = 1.0 / math.sqrt(q1.shape[-1])

    def blk(q0, q1_):
        rel = jnp.arange(q1_)[None, :] - jnp.arange(q0, q1_)[:, None]
        bias = jnp.transpose(rel_bias[_t5_bucket(rel)], (2, 0, 1)).astype(jnp.float32)
        mask = _chunk_mask(q0, q1_)
        s1 = jnp.einsum('bqhd,bkhd->bhqk', q1[:, q0:q1_], k1[:, :q1_]).astype(jnp.float32) * scale + bias
        s2 = jnp.einsum('bqhd,bkhd->bhqk', q2[:, q0:q1_], k2[:, :q1_]).astype(jnp.float32) * scale + bias
        p1 = jax.nn.softmax(jnp.where(mask, s1, -jnp.inf), axis=-1)
        p2 = jax.nn.softmax(jnp.where(mask, s2, -jnp.inf), axis=-1)
        w = (p1 - lam * p2).astype(v.dtype)
        return jnp.einsum('bhqk,bkhd->bqhd', w, v[:, :q1_])

    return _sweep(blk, q1.shape[1])


def setup_inputs(seed: int = 0) -> dict:
    key = jax.random.key(seed)
    ks = jax.random.split(key, 20)
    nrm = lambda k, shape, s: jax.random.normal(k, shape, jnp.float32) * s
    return {
        "x": nrm(ks[0], (BATCH, SEQ, D_MODEL), 1.0),
        "norm_g": 1.0 + nrm(ks[1], (DEPTH, D_MODEL), 0.02),
        "w_in": nrm(ks[2], (DEPTH, D_MODEL, N_IN), D_MODEL ** -0.5),
        "b_forget": jax.random.uniform(ks[3], (DEPTH, A_HEADS), jnp.float32, 1.0, 4.0),
        "mla_q_norm_g": 1.0 + nrm(ks[4], (DEPTH, B_Q_LORA), 0.02),
        "w_uq": nrm(ks[5], (DEPTH, B_Q_LORA, B_HEADS * (B_NOPE + B_ROPE)), B_Q_LORA ** -0.5),
        "mla_kv_norm_g": 1.0 + nrm(ks[6], (DEPTH, B_KV_LORA), 0.02),
        "w_ukv": nrm(ks[7], (DEPTH, B_KV_LORA, B_HEADS * (B_NOPE + B_V)), B_KV_LORA ** -0.5),
        "lambda_q1": nrm(ks[8], (DEPTH, C_QK), 0.1),
        "lambda_k1": nrm(ks[9], (DEPTH, C_QK), 0.1),
        "lambda_q2": nrm(ks[10], (DEPTH, C_QK), 0.1),
        "lambda_k2": nrm(ks[11], (DEPTH, C_QK), 0.1),
        "diff_subln_g": 1.0 + nrm(ks[12], (DEPTH, C_V), 0.02),
        "rel_bias": nrm(ks[13], (REL_BUCKETS, C_HEADS), 0.5),
        "w_out": nrm(ks[14], (DEPTH, D_MIX, D_MODEL), D_MIX ** -0.5),
        "final_norm_g": 1.0 + nrm(ks[15], (D_MODEL,), 0.02),
    }


def reference(x, norm_g, w_in, b_forget, mla_q_norm_g, w_uq, mla_kv_norm_g, w_ukv,
              lambda_q1, lambda_k1, lambda_q2, lambda_k2, diff_subln_g, rel_bias,
              w_out, final_norm_g):
    Bsz, S, _ = x.shape
    pos = jnp.arange(S)
    for l in range(DEPTH):
        h = _rms_norm(x, norm_g[l])
        proj = jnp.einsum('bsd,dn->bsn', h, w_in[l])
        (a_q, a_k, a_v, a_f, a_gate,
         b_cq, b_ckv, b_krope, b_gate,
         c_q, c_k, c_v, c_gate) = jnp.split(proj, IN_OFFSETS, axis=-1)

        logf = jax.nn.log_sigmoid((a_f + b_forget[l]).astype(jnp.float32))
        cum_logf = jnp.cumsum(logf, axis=1)
        a_out = _fox_attention(a_q.reshape(Bsz, S, A_HEADS, HEAD_DIM),
                               a_k.reshape(Bsz, S, A_HEADS, HEAD_DIM),
                               a_v.reshape(Bsz, S, A_HEADS, HEAD_DIM), cum_logf)
        a_out = a_out.reshape(Bsz, S, A_DIM) * jax.nn.silu(a_gate)

        q = jnp.einsum('bsr,rn->bsn', _rms_norm(b_cq, mla_q_norm_g[l]), w_uq[l])
        q = q.reshape(Bsz, S, B_HEADS, B_NOPE + B_ROPE)
        q_nope, q_rope = q[..., :B_NOPE], _rope(q[..., B_NOPE:], pos)
        kv = jnp.einsum('bsr,rn->bsn', _rms_norm(b_ckv, mla_kv_norm_g[l]), w_ukv[l])
        kv = kv.reshape(Bsz, S, B_HEADS, B_NOPE + B_V)
        k_nope, b_v = kv[..., :B_NOPE], kv[..., B_NOPE:]
        k_rope = _rope(b_krope[:, :, None, :], pos)[:, :, 0, :]
        b_out = _mla_attention(q_nope, q_rope, k_nope, k_rope, b_v)
        b_out = b_out.reshape(Bsz, S, B_DIM) * jax.nn.silu(b_gate)

        lam_init = 0.8 - 0.6 * math.exp(-0.3 * l)
        lam = (jnp.exp(jnp.sum(lambda_q1[l].astype(jnp.float32) * lambda_k1[l].astype(jnp.float32)))
               - jnp.exp(jnp.sum(lambda_q2[l].astype(jnp.float32) * lambda_k2[l].astype(jnp.float32)))
               + lam_init)
        cq = c_q.reshape(Bsz, S, C_HEADS, 2, C_QK)
        ck = c_k.reshape(Bsz, S, C_HEADS, 2, C_QK)
        c_out = _diff_attention(cq[..., 0, :], cq[..., 1, :], ck[..., 0, :], ck[..., 1, :],
                                c_v.reshape(Bsz, S, C_HEADS, C_V), lam, rel_bias)
        c_out = _rms_norm(c_out, diff_subln_g[l]) * (1.0 - lam_init)
        c_out = c_out.reshape(Bsz, S, C_DIM) * jax.nn.silu(c_gate)

        mixed = jnp.concatenate([a_out, b_out, c_out], axis=-1)
        x = x + jnp.einsum('bsm,md->bsd', mixed, w_out[l])
    return _rms_norm(x, final_norm_g)
```

```python
import contextlib
import math
import os

import numpy as np
import ml_dtypes

import concourse.bass as bass
import concourse.mybir as mybir
from concourse.bass_utils import run_bass_kernel_spmd

F32 = mybir.dt.float32
BF16 = mybir.dt.bfloat16
AF = mybir.ActivationFunctionType
ALU = mybir.AluOpType
AX = mybir.AxisListType

D = 2048
S = 4096
DEPTH = 2
NIN = 6726
EPS = 1e-6
NEG = -30000.0
KSKIP = os.environ.get('KSKIP', '').split(',')
SILU_FUNC = AF.Copy if os.environ.get('KDBG_SILU') == 'copy' else AF.Silu
NT = S // 128
TS = 1024
O_AQ, O_AK, O_AV, O_AF, O_AG = 0, 768, 1536, 2304, 2310
O_BCQ, O_BCKV, O_BKR, O_BG = 3078, 3590, 3846, 3910
O_CQ, O_CK, O_CV, O_CG = 4678, 5190, 5702, 6214


class Buf:
    __slots__ = ("name", "w", "rs", "rd")

    def __init__(self, name=""):
        self.name = name
        self.w = None
        self.rs = {}
        self.rd = []


class Sched:
    CE = ("pe", "act", "dve", "pool")

    def __init__(self, nc, stack):
        self.nc = nc
        self.E = {"pe": nc.tensor, "act": nc.scalar, "dve": nc.vector, "pool": nc.gpsimd, "sp": nc.sync}
        self.sem = {k: stack.enter_context(nc.semaphore("s_" + k)) for k in self.CE}
        self.cnt = {k: 0 for k in self.CE}
        self.pending = {k: [] for k in self.CE}
        self.waited = {k: {} for k in self.E}
        self.KD = 8
        self.dq = ("sp", "act")
        self.dsem = {q: [stack.enter_context(nc.semaphore(f"d_{q}{i}")) for i in range(self.KD)] for q in self.dq}
        self.dcnt = {q: 0 for q in self.dq}
        self.dtoks = {q: [None] * self.KD for q in self.dq}
        self.last = {}
        self.nwaits = 0

    def _wait(self, e, tok):
        if tok is None:
            return
        key, sem, val, _ = tok
        assert val is not None, "dependency on an unsignaled instruction"
        w = self.waited[e]
        if w.get(key, 0) >= val:
            return
        self.E[e].wait_ge(sem, val)
        self.nwaits += 1
        w[key] = val

    def _deps(self, e, reads, writes):
        for b in reads:
            if b.w is not None:
                self._wait(e, b.w)
        for b in writes:
            if b.w is not None and not (b.w[3] == e and e == "pe"):
                self._wait(e, b.w)
            for en, t in b.rs.items():
                if not (en == e and e == "pe"):
                    self._wait(e, t)
            for t in b.rd:
                self._wait(e, t)

    def _record(self, e, tok, reads, writes, is_dma):
        for b in reads:
            if is_dma:
                b.rd.append(tok)
                if len(b.rd) > 64:
                    b.rd = b.rd[-64:]
            else:
                b.rs[e] = tok
        for b in writes:
            b.w = tok
            b.rs = {}
            b.rd = []

    def op(self, e, fn, reads=(), writes=(), sig=True):
        self._deps(e, reads, writes)
        ins = fn(self.E[e])
        tok = [e, self.sem[e], None, e]
        if sig:
            self.cnt[e] += 1
            ins.then_inc(self.sem[e], 1)
            tok[2] = self.cnt[e]
            for p in self.pending[e]:
                p[2] = self.cnt[e]
            self.pending[e] = []
        else:
            self.pending[e].append(tok)
        self._record(e, tok, reads, writes, False)
        self.last[e] = tok
        return tok

    def dma(self, out, in_, reads=(), writes=(), q="sp", **kw):
        for b in reads:
            if b.w is not None:
                self._wait(q, b.w)
        for b in writes:
            if b.w is not None:
                self._wait(q, b.w)
            for t in b.rs.values():
                self._wait(q, t)
            for t in b.rd:
                self._wait(q, t)
        j = self.dcnt[q]
        slot = j % self.KD
        self._wait(q, self.dtoks[q][slot])
        ins = self.E[q].dma_start(out=out, in_=in_, **kw)
        ins.then_inc(self.dsem[q][slot], 16)
        tok = [f"d_{q}{slot}", self.dsem[q][slot], 16 * (j // self.KD + 1), "dma"]
        self.dtoks[q][slot] = tok
        self.dcnt[q] = j + 1
        self._record(q, tok, reads, writes, True)
        return tok

    def barrier(self):
        for e in self.CE:
            assert not self.pending[e], f"pending unsignaled instructions on {e} at barrier"
        toks = [self.last[e] for e in self.CE if e in self.last]
        for q in self.dq:
            toks += [t for t in self.dtoks[q] if t is not None]
        for e in self.E:
            for t in toks:
                self._wait(e, t)

    def finish(self):
        toks = []
        for q in self.dq:
            toks += [t for t in self.dtoks[q] if t is not None]
        for t in toks:
            self._wait("sp", t)


class Ring:
    def __init__(self, tiles):
        self.tiles = tiles
        self.bufs = [Buf() for _ in tiles]
        self.i = 0

    def next(self):
        k = self.i % len(self.tiles)
        self.i += 1
        return self.tiles[k], self.bufs[k]


def build_program(n_layers=DEPTH, debug=False, stop_after=None, n_sc=S // TS, n_grp=None):
    nc = bass.Bass("TRN2", target_bir_lowering=False)
    stack = contextlib.ExitStack()
    sc = Sched(nc, stack)

    def din(name, shape, dt=F32):
        return nc.dram_tensor(name, list(shape), dt, kind="ExternalInput").ap()

    dbg_names = set(debug) if debug else set()

    def dscr(name, shape, dt):
        kind = "ExternalOutput" if name in dbg_names else "Internal"
        return nc.dram_tensor(name, list(shape), dt, kind=kind).ap()

    x_in = din("x", [S, D])
    norm_g = din("norm_g", [DEPTH, D])
    w_in = din("w_in", [DEPTH, D, NIN])
    b_forget = din("b_forget", [DEPTH, 6])
    mla_q_g = din("mla_q_norm_g", [DEPTH, 512])
    w_uq = din("w_uq", [DEPTH, 512, 1152])
    mla_kv_g = din("mla_kv_norm_g", [DEPTH, 256])
    w_ukv = din("w_ukv", [DEPTH, 256, 1536])
    lq1 = din("lambda_q1", [DEPTH, 64])
    lk1 = din("lambda_k1", [DEPTH, 64])
    lq2 = din("lambda_q2", [DEPTH, 64])
    lk2 = din("lambda_k2", [DEPTH, 64])
    subln_g = din("diff_subln_g", [DEPTH, 128])
    rel_bias = din("rel_bias", [32, 4])
    w_out = din("w_out", [DEPTH, D, D])
    final_g = din("final_norm_g", [D])
    c_ident = din("c_ident", [128, 128], BF16)
    c_identf = din("c_identf", [128, 128], F32)
    c_ones = din("c_ones", [128, 128], BF16)
    c_cos = din("c_cos2", [64, S], F32)
    c_sin = din("c_sin2", [64, S], F32)
    c_maskA = din("c_maskA", [128, 128], F32)
    c_maskB = din("c_maskB", [128, 128], F32)
    c_bkt = din("c_bkt", [128, 32, 256], BF16)
    c_sel = din("c_sel", [6, 768], F32)
    c_role = din("c_role", [128, 2], F32)
    c_mA2 = din("c_mA2", [128, 256], F32)
    c_mB2 = din("c_mB2", [128, 256], F32)
    c_bkt3 = din("c_bkt3", [128, 32, 384], BF16)
    c_add3 = din("c_add3", [128, 384], F32)
    c_add2 = din("c_add2", [128, 256], F32)
    c_cos_o = din("c_cos_own", [64, S // 2], F32)
    c_sin_o = din("c_sin_own", [64, S // 2], F32)

    y_out = nc.dram_tensor("y", [S // 2, D], F32, kind="ExternalOutput").ap()

    X1 = dscr("X1", [S, D], F32)
    QA = dscr("QA", [6, 128, S], BF16)
    KA = dscr("KA", [6, 128, S], BF16)
    VA = dscr("VA", [S, 768], BF16)
    GA = dscr("GA", [6, 128, S], F32)
    NLF = dscr("NLF", [6, S], F32)
    QBN = dscr("QBN", [6, 128, S], BF16)
    QBR = dscr("QBR", [6, 64, S], BF16)
    KBN = dscr("KBN", [6, 128, S], BF16)
    KBR = dscr("KBR", [64, S], BF16)
    VB = dscr("VB", [S, 768], BF16)
    GB = dscr("GB", [6, 128, S], F32)
    QC = dscr("QC", [4, 128, S], BF16)
    KC = dscr("KC", [4, 128, S], BF16)
    VC = dscr("VC", [S, 512], BF16)
    GC = dscr("GC", [4, 128, S], F32)
    MIX = dscr("MIX", [16, 128, S], BF16)
    WPK = dscr("WPK", [40, 128, 16 * 256], BF16)
    dB = {n: Buf(n) for n in ("X1", "QA", "KA", "VA", "GA", "NLF", "QBN", "QBR", "KBN", "KBR", "VB", "GB",
                              "QC", "KC", "VC", "GC", "MIX", "Y")}
    bWPK = [Buf(f"WPK{i}") for i in range(40)]

    def sb(name, shape, dt):
        return stack.enter_context(nc.sbuf_tensor(name, list(shape), dt))

    ident = sb("ident", [128, 128], BF16)
    identf = sb("identf", [128, 128], F32)
    ones = sb("ones", [128, 128], BF16)
    maskA = sb("maskA", [128, 128], F32)
    maskB = sb("maskB", [128, 128], F32)
    sel = sb("sel", [6, 768], F32)
    rb_bc = sb("rb_bc", [128, 128], F32)
    role = sb("role", [128, 2], F32)
    bK = Buf("consts")
    for t, src in ((ident, c_ident), (identf, c_identf), (ones, c_ones), (maskA, c_maskA), (maskB, c_maskB)):
        sc.dma(t[:], src[:, :], writes=[bK])
    sc.dma(sel[:], c_sel[:, :], writes=[bK])
    sc.dma(role[:], c_role[:, :], writes=[bK])
    sc.dma(rb_bc[:], rel_bias.rearrange("a b -> (a b)").partition_broadcast(128), writes=[bK])

    if stop_after == ("S", 0):
        sc.finish()
        stack.close()
        return nc, sc

    evsel = [0]

    def ev_engine():
        evsel[0] ^= 1
        return "act" if evsel[0] else "dve"

    def copy_op(eng, out, in_):
        if eng == "act":
            return lambda e: e.copy(out=out, in_=in_)
        return lambda e: e.tensor_copy(out=out, in_=in_)

    for l in range(n_layers):
        xsrc = x_in if l == 0 else X1
        xsrc_b = Buf("xin") if l == 0 else dB["X1"]
        last = (l == DEPTH - 1)
        split = last
        NOWN = NT // 2 if split else NT
        ra = role[:, 0:1]
        rb_ = role[:, 1:2]
        lam_init = 0.8 - 0.6 * math.exp(-0.3 * l)

        with contextlib.ExitStack() as ps:
            def psb(name, shape, dt):
                return ps.enter_context(nc.sbuf_tensor(f"{name}_L{l}", list(shape), dt))

            def ppm(name, shape, dt=F32):
                return ps.enter_context(nc.psum_tensor(f"{name}_L{l}", list(shape), dt))

            g_bc = psb("g_bc", [128, D], F32)
            gq = psb("gq", [128, 4], F32)
            gkv = psb("gkv", [128, 2], F32)
            nbf = psb("nbf", [6, 1], F32)
            wuq_b = psb("wuq_b", [128, 4, 1536], BF16)
            wukv_b = psb("wukv_b", [128, 2, 1536], BF16)
            bW = Buf("pw")
            sc.dma(g_bc[:], norm_g[l:l + 1, :].partition_broadcast(128), writes=[bW])
            with nc.allow_non_contiguous_dma(reason="tiny param vectors"):
                sc.dma(gq[:], mla_q_g[l, :].rearrange("(k p) -> p k", p=128), writes=[bW])
                sc.dma(gkv[:], mla_kv_g[l, :].rearrange("(k p) -> p k", p=128), writes=[bW])
                sc.dma(nbf[:], b_forget[l, :].rearrange("(p o) -> p o", o=1), writes=[bW])
            sc.op("dve", lambda e: e.tensor_scalar(out=nbf[:], in0=nbf[:], scalar1=-1.0, scalar2=None, op0=ALU.mult),
                  reads=[bW], writes=[bW])
            with nc.sbuf_tensor(f"wuq_st_L{l}", [128, 4, 1152], F32) as wuq_st, \
                    nc.sbuf_tensor(f"wukv_st_L{l}", [128, 2, 1536], F32) as wukv_st:
                bst = Buf("wst0")
                sc.dma(wuq_st[:], w_uq[l].rearrange("(k p) n -> p k n", p=128), writes=[bst])
                sc.dma(wukv_st[:], w_ukv[l].rearrange("(k p) n -> p k n", p=128), writes=[bst])
                bW1, bW2 = Buf("pw1"), Buf("pw2")
                for h in range(6):
                    for (s0, s1, d0) in ((0, 192, 0), (160, 192, 192), (128, 160, 224)):
                        sc.op("act", lambda e, h=h, s0=s0, s1=s1, d0=d0: e.copy(
                            out=wuq_b[:, :, h * 256 + d0:h * 256 + d0 + (s1 - s0)],
                            in_=wuq_st[:, :, h * 192 + s0:h * 192 + s1]), reads=[bst], writes=[bW1])
                    sc.op("dve", lambda e, h=h: e.tensor_copy(
                        out=wukv_b[:, :, h * 128:h * 128 + 128], in_=wukv_st[:, :, h * 256:h * 256 + 128]),
                        reads=[bst], writes=[bW2])
                    sc.op("dve", lambda e, h=h: e.tensor_copy(
                        out=wukv_b[:, :, 768 + h * 128:768 + h * 128 + 128],
                        in_=wukv_st[:, :, h * 256 + 128:h * 256 + 256]), reads=[bst], writes=[bW2])
                sc.barrier()

            hT = psb("hT", [128, 16, TS], BF16)
            bhT = Buf("hT")
            wst = Ring([psb(f"wst{i}", [128, 16, 256], F32) for i in range(2)])
            wbr = Ring([psb(f"wb{i}", [128, 16, 256], BF16) for i in range(2)])
            wbr2 = [Buf("wb2_0"), Buf("wb2_1")]
            xst = Ring([psb(f"xst{i}", [128, D], F32) for i in range(2)])
            xs = psb("xs", [128, D], BF16)
            bxs = Buf("xs")
            ssq = Ring([psb(f"ssq{i}", [128, 1], F32) for i in range(2)])
            rstd = Ring([psb(f"rstd{i}", [128, 1], F32) for i in range(2)])
            st16 = Ring([psb(f"st16_{i}", [128, 512], BF16) for i in range(4)])
            st32 = Ring([psb(f"st32_{i}", [128, 512], F32) for i in range(3)])
            vst = Ring([psb(f"vst{i}", [128, 384], BF16) for i in range(3)])
            RAWN = 2048 if split else 4096
            raw = psb("raw", [128, RAWN], F32)
            braw = Buf("raw")
            sqr = Ring([psb(f"sq{i}", [128, 512], BF16) for i in range(2)])
            rbc = psb("rstd_bc", [128, TS], F32)
            brbc = Buf("rbc")
            rtmp = psb("rtmp", [128, 512], F32)
            brtmp = Buf("rtmp")
            cn = psb("cn", [128, RAWN], BF16)
            bcn = Buf("cn")
            cos_t = psb("cos_t", [64, TS], F32)
            sin_t = psb("sin_t", [64, TS], F32)
            bcs = Buf("cs")
            rp = Ring([psb(f"rp{i}", [64, 512], F32) for i in range(2)])
            if split:
                hTo = psb("hTo", [128, 16, 512], BF16)
                cos_o = psb("cos_o", [64, 512], F32)
                sin_o = psb("sin_o", [64, 512], F32)
            bhTo = Buf("hTo")

            def rawv(t, si, hs, nsub):
                st_ = RAWN // nsub
                return t[:, si * st_ + hs.start:si * st_ + hs.stop]
            aft = Ring([psb(f"aft{i}", [6, 512], F32) for i in range(2)])

            mp = Ring([ppm(f"mp{i}", [128, 512]) for i in range(4)])
            Rp = [ppm(f"Rp{i}", [128, 512]) for i in range(2)]
            bRp = [Buf("Rp0"), Buf("Rp1")]
            tp = Ring([ppm(f"tp{i}", [128, 512], BF16) for i in range(2)])

            groups = []
            for h0 in (0, 256, 512):
                groups.append((O_AQ + h0, 256, "fm16q", (QA, "QA", h0 // 128)))
            for h0 in (0, 256, 512):
                groups.append((O_AK + h0, 256, "fm16", (KA, "KA", h0 // 128)))
            for h0 in (0, 256, 512):
                groups.append((O_AV + h0, 256, "tm", (VA, "VA", h0)))
            groups.append((O_AF, 6, "af", None))
            for h0 in (0, 256, 512):
                groups.append((O_AG + h0, 256, "silu", (GA, "GA", h0 // 128)))
            groups.append((O_BCQ, 256, "cq", 0))
            groups.append((O_BCQ + 256, 256, "cq", 2))
            groups.append((O_BCKV, 256, "ckv", 0))
            groups.append((O_BKR, 128, "krope", None))
            for h0 in (0, 256, 512):
                groups.append((O_BG + h0, 256, "silu", (GB, "GB", h0 // 128)))
            for h0 in (0, 256):
                groups.append((O_CQ + h0, 256, "fm16q", (QC, "QC", h0 // 128)))
            for h0 in (0, 256):
                groups.append((O_CK + h0, 256, "fm16", (KC, "KC", h0 // 128)))
            for h0 in (0, 256):
                groups.append((O_CV + h0, 256, "tm", (VC, "VC", h0)))
            for h0 in (0, 256):
                groups.append((O_CG + h0, 256, "silu", (GC, "GC", h0 // 128)))

            w_l = w_in[l]

            def load_group(gi):
                c0, W, kind, _ = groups[gi]
                t, b = wst.next()
                if kind == "krope":
                    sc.dma(t[:, :, 0:64], w_l[:, c0:c0 + 64].rearrange("(k p) n -> p k n", p=128), writes=[b])
                    sc.dma(t[:, :, 64:96], w_l[:, c0 + 32:c0 + 64].rearrange("(k p) n -> p k n", p=128), writes=[b])
                    sc.dma(t[:, :, 96:128], w_l[:, c0:c0 + 32].rearrange("(k p) n -> p k n", p=128), writes=[b])
                else:
                    sc.dma(t[:, :, 0:W], w_l[:, c0:c0 + W].rearrange("(k p) n -> p k n", p=128), writes=[b])
                return t, b

            def fm_matmuls(pbank, pbuf, lhs_fn, nk, rhs_fn, m, ncols=512):
                for k in range(nk):
                    sc.op("pe", lambda e, k=k: e.matmul(pbank[0:m, 0:ncols], lhsT=lhs_fn(k), rhs=rhs_fn(k),
                                                         start=(k == 0), stop=(k == nk - 1)),
                          reads=[bhT, bhTo, bcn, bW], writes=[pbuf], sig=(k == nk - 1))

            def rope_combine(pa, ba, pb_, bb, cos_ap, sin_ap, dst_ap, dst_buf):
                t1, b1 = rp.next()
                t2, b2 = rp.next()
                sc.op("dve", lambda e: e.tensor_tensor(out=t1[:], in0=pa[0:64, :], in1=cos_ap, op=ALU.mult),
                      reads=[ba, bcs], writes=[b1])
                sc.op("dve", lambda e: e.tensor_tensor(out=t2[:], in0=pb_[0:64, :], in1=sin_ap, op=ALU.mult),
                      reads=[bb, bcs], writes=[b2])
                so, bo = st16.next()
                sc.op("pool", lambda e: e.tensor_tensor(out=so[0:64, :], in0=t1[:], in1=t2[:], op=ALU.add),
                      reads=[b1, b2], writes=[bo])
                sc.dma(dst_ap, so[0:64, :], reads=[bo], writes=[dst_buf])

            for sci in range(n_sc):
                tok0 = sci * TS
                ngroups = len(groups) if n_grp is None else n_grp
                packed = (sci > 0) and (n_grp is None)
                if packed:
                    def load_packed(gi):
                        t, b = wbr.next()
                        b2 = wbr2[(wbr.i - 1) % 2]
                        sc.dma(t[:], WPK[gi].rearrange("p (k n) -> p k n", k=16), reads=[bWPK[gi]], writes=[b, b2])
                        return t, b, b2
                    pre = [load_packed(0), load_packed(1)]
                else:
                    pre = [load_group(0), load_group(1)]
                sc.dma(cos_t[:], c_cos[:, tok0:tok0 + TS], writes=[bcs])
                sc.dma(sin_t[:], c_sin[:, tok0:tok0 + TS], writes=[bcs])
                xl = [None] * 8
                t, b = xst.next()
                sc.dma(t[:], xsrc[tok0:tok0 + 128, :], reads=[xsrc_b], writes=[b])
                xl[0] = (t, b)
                for tt in range(8):
                    if tt + 1 < 8:
                        t, b = xst.next()
                        r0 = tok0 + (tt + 1) * 128
                        sc.dma(t[:], xsrc[r0:r0 + 128, :], reads=[xsrc_b], writes=[b])
                        xl[tt + 1] = (t, b)
                    xt, bx = xl[tt]
                    sq_t, bsq = ssq.next()
                    rs_t, brs = rstd.next()
                    sc.op("act", lambda e, xt=xt, sq_t=sq_t: e.activation(out=xs[:], in_=xt[:], func=AF.Square,
                                                                           accum_out=sq_t[:]),
                          reads=[bx], writes=[bxs, bsq])
                    sc.op("act", lambda e, sq_t=sq_t: e.activation(out=sq_t[:], in_=sq_t[:], func=AF.Sqrt,
                                                                   scale=1.0 / D, bias=EPS),
                          reads=[bsq], writes=[bsq])
                    sc.op("dve", lambda e, sq_t=sq_t, rs_t=rs_t: e.reciprocal(out=rs_t[:], in_=sq_t[:]),
                          reads=[bsq], writes=[brs])
                    sc.op("dve", lambda e, xt=xt, rs_t=rs_t: e.scalar_tensor_tensor(
                        out=xs[:], in0=xt[:], scalar=rs_t[:, 0:1], in1=g_bc[:], op0=ALU.mult, op1=ALU.mult),
                        reads=[bx, brs, bW], writes=[bxs])
                    for q4 in range(4):
                        tpt, btp = tp.next()
                        for i in range(4):
                            dt_ = q4 * 4 + i
                            sc.op("pe", lambda e, dt_=dt_, i=i, tpt=tpt: e.transpose(
                                out=tpt[:, i * 128:(i + 1) * 128], in_=xs[:, dt_ * 128:(dt_ + 1) * 128],
                                identity=ident[:]),
                                reads=[bxs, bK], writes=[btp], sig=(i == 3))
                        eng = ev_engine()
                        sc.op(eng, copy_op(eng, hT[:, q4 * 4:q4 * 4 + 4, tt * 128:(tt + 1) * 128],
                                           tpt[:, 0:512].rearrange("p (a b) -> p a b", a=4)),
                              reads=[btp], writes=[bhT])

                if split:
                    sc.dma(cos_o[:], c_cos_o[:, sci * 512:sci * 512 + 512], writes=[bcs])
                    sc.dma(sin_o[:], c_sin_o[:, sci * 512:sci * 512 + 512], writes=[bcs])
                    for p4 in range(4):
                        sc.op("dve", lambda e, p4=p4: e.tensor_scalar(
                            out=hTo[:, :, p4 * 128:p4 * 128 + 128], in0=hT[:, :, 2 * p4 * 128:2 * p4 * 128 + 128],
                            scalar1=ra, scalar2=None, op0=ALU.mult), reads=[bhT, bK], writes=[bhTo])
                        sc.op("dve", lambda e, p4=p4: e.scalar_tensor_tensor(
                            out=hTo[:, :, p4 * 128:p4 * 128 + 128],
                            in0=hT[:, :, (2 * p4 + 1) * 128:(2 * p4 + 1) * 128 + 128], scalar=rb_,
                            in1=hTo[:, :, p4 * 128:p4 * 128 + 128], op0=ALU.mult, op1=ALU.add),
                            reads=[bhT, bhTo, bK], writes=[bhTo])

                def halves(side):
                    if split and side == "q":
                        return [(0, (lambda k: hTo[:, k, 0:512]), sci * 512, slice(0, 512), cos_o[:, :], sin_o[:, :])]
                    out_ = []
                    for half in range(2):
                        hs_ = slice(half * 512, half * 512 + 512)
                        out_.append((half, (lambda k, hs_=hs_: hT[:, k, hs_]), tok0 + half * 512, hs_,
                                     cos_t[:, hs_], sin_t[:, hs_]))
                    return out_

                for gi in range(ngroups):
                    c0, W, kind, info = groups[gi]
                    if packed:
                        wb_t, bwb, bwb2 = pre[gi % 2]
                    else:
                        wt, bwt = pre[gi % 2]
                        wb_t, bwb = wbr.next()
                        bwb2 = wbr2[(wbr.i - 1) % 2]
                        Wc = 128 if kind == "krope" else W
                        sc.op("act", lambda e, wt=wt, wb_t=wb_t, Wc=Wc: e.copy(out=wb_t[:, 0:8, 0:Wc],
                                                                                in_=wt[:, 0:8, 0:Wc]),
                              reads=[bwt], writes=[bwb])
                        sc.op("dve", lambda e, wt=wt, wb_t=wb_t, Wc=Wc: e.tensor_copy(out=wb_t[:, 8:16, 0:Wc],
                                                                                        in_=wt[:, 8:16, 0:Wc]),
                              reads=[bwt], writes=[bwb2])
                        if n_grp is None:
                            sc.dma(WPK[gi].rearrange("p (k n) -> p k n", k=16), wb_t[:], reads=[bwb, bwb2],
                                   writes=[bWPK[gi]])
                        if gi + 2 < ngroups:
                            pre[gi % 2] = load_group(gi + 2)

                    def main_mm(pb_, bpb, c_lo, m, rhs_fn, wb_t=wb_t, bwb=bwb, bwb2=bwb2):
                        for k in range(16):
                            sc.op("pe", lambda e, k=k: e.matmul(
                                pb_[0:m, :], lhsT=wb_t[:, k, c_lo:c_lo + m], rhs=rhs_fn(k),
                                start=(k == 0), stop=(k == 15)),
                                reads=[bhT, bhTo, bwb, bwb2], writes=[bpb], sig=(k == 15))

                    if kind in ("fm16", "fm16q", "silu"):
                        dst, dname, hb = info
                        side = "kv" if kind == "fm16" else "q"
                        for sub in range(W // 128):
                            for (hi, rhs_fn, t0, hs, _c, _s) in halves(side):
                                pb_, bpb = mp.next()
                                main_mm(pb_, bpb, sub * 128, 128, rhs_fn)
                                dsl = dst[hb + sub, :, t0:t0 + 512]
                                if kind != "silu":
                                    so, bo = st16.next()
                                    eng = ev_engine()
                                    sc.op(eng, copy_op(eng, so[:], pb_[:, :]), reads=[bpb], writes=[bo])
                                else:
                                    s1, bs1 = st32.next()
                                    sc.op("act", lambda e, s1=s1, pb_=pb_: e.activation(out=s1[:], in_=pb_[:, :],
                                                                                         func=AF.Exp, scale=-1.0),
                                          reads=[bpb], writes=[bs1])
                                    sc.op("act", lambda e, s1=s1: e.activation(out=s1[:], in_=s1[:], func=AF.Ln,
                                                                               bias=1.0),
                                          reads=[bs1], writes=[bs1])
                                    sc.op("act", lambda e, s1=s1: e.activation(out=s1[:], in_=s1[:], func=AF.Exp,
                                                                               scale=-1.0),
                                          reads=[bs1], writes=[bs1])
                                    so, bo = st32.next()
                                    sc.op("dve", lambda e, so=so, s1=s1, pb_=pb_: e.tensor_tensor(
                                        out=so[:], in0=pb_[:, :], in1=s1[:], op=ALU.mult),
                                        reads=[bpb, bs1], writes=[bo])
                                sc.dma(dsl, so[:], reads=[bo], writes=[dB[dname]])
                    elif kind == "tm":
                        dst, dname, cb = info
                        for tt in range(8):
                            pb_, bpb = mp.next()
                            for k in range(16):
                                sc.op("pe", lambda e, k=k, pb_=pb_, tt=tt, wb_t=wb_t: e.matmul(
                                    pb_[:, 0:256], lhsT=hT[:, k, tt * 128:tt * 128 + 128], rhs=wb_t[:, k, 0:256],
                                    start=(k == 0), stop=(k == 15)),
                                    reads=[bhT, bwb, bwb2], writes=[bpb], sig=(k == 15))
                            so, bo = vst.next()
                            eng = ev_engine()
                            sc.op(eng, copy_op(eng, so[:, 0:256], pb_[:, 0:256]), reads=[bpb], writes=[bo])
                            r0 = tok0 + tt * 128
                            sc.dma(dst[r0:r0 + 128, cb:cb + 256], so[:, 0:256], reads=[bo], writes=[dB[dname]])
                    elif kind == "af":
                        for (hi, rhs_fn, t0, hs, _c, _s) in halves("kv"):
                            pb_, bpb = mp.next()
                            main_mm(pb_, bpb, 0, 6, rhs_fn)
                            e1, be1 = aft.next()
                            e2, be2 = aft.next()
                            sc.op("act", lambda e, e1=e1, pb_=pb_: e.activation(
                                out=e1[:], in_=pb_[0:6, :], func=AF.Exp, scale=-1.0, bias=nbf[:, 0:1]),
                                reads=[bpb, bW], writes=[be1])
                            sc.op("act", lambda e, e1=e1, e2=e2: e.activation(
                                out=e2[:], in_=e1[:], func=AF.Ln, bias=1.0), reads=[be1], writes=[be2])
                            sc.dma(NLF[:, t0:t0 + 512], e2[:], reads=[be2], writes=[dB["NLF"]])
                    elif kind in ("cq", "ckv"):
                        sub0 = info
                        nr = 4 if kind == "cq" else 2
                        gvec = gq if kind == "cq" else gkv
                        side = "q" if kind == "cq" else "kv"
                        hv = halves(side)
                        defR = []
                        for sub in range(2):
                            si = sub0 + sub
                            for (hi, rhs_fn, t0, hs, _c, _s) in hv:
                                pb_, bpb = mp.next()
                                main_mm(pb_, bpb, sub * 128, 128, rhs_fn)
                                for fn_ in defR:
                                    fn_()
                                defR = []
                                sc.op("dve", lambda e, pb_=pb_, si=si, hs=hs: e.tensor_copy(
                                    out=rawv(raw, si, hs, nr), in_=pb_[:, :]), reads=[bpb], writes=[braw])
                                sq_t, bsq = sqr.next()
                                sc.op("act", lambda e, si=si, hs=hs, sq_t=sq_t: e.activation(
                                    out=sq_t[:], in_=rawv(raw, si, hs, nr), func=AF.Square),
                                    reads=[braw], writes=[bsq])
                                defR.append(lambda sq_t=sq_t, bsq=bsq, hi=hi, si=si: sc.op(
                                    "pe", lambda e: e.matmul(
                                        Rp[hi][:, :], lhsT=ones[:], rhs=sq_t[:], start=(si == 0), stop=(si == nr - 1)),
                                    reads=[bsq, bK], writes=[bRp[hi]], sig=True))
                        for fn_ in defR:
                            fn_()
                        if sub0 + 2 == nr:
                            for (hi, rhs_fn, t0, hs, _c, _s) in hv:
                                sc.op("act", lambda e, hi=hi: e.activation(
                                    out=rtmp[:], in_=Rp[hi][:, :], func=AF.Sqrt, scale=1.0 / (128 * nr), bias=EPS),
                                    reads=[bRp[hi]], writes=[brtmp])
                                sc.op("dve", lambda e, hs=hs: e.reciprocal(out=rbc[:, hs], in_=rtmp[:]),
                                      reads=[brtmp], writes=[brbc])
                                for si in range(nr):
                                    sc.op("dve", lambda e, si=si, hs=hs: e.scalar_tensor_tensor(
                                        out=rawv(cn, si, hs, nr), in0=rawv(raw, si, hs, nr), scalar=gvec[:, si:si + 1],
                                        in1=rbc[:, hs], op0=ALU.mult, op1=ALU.mult),
                                        reads=[braw, brbc, bW], writes=[bcn])
                            if kind == "cq":
                                for h in range(6):
                                    for (hi, rhs_fn, t0, hs, cos_ap, sin_ap) in hv:
                                        ts_ = slice(t0, t0 + 512)
                                        pb_, bpb = mp.next()
                                        fm_matmuls(pb_, bpb, lambda k, h=h: wuq_b[:, k, h * 256:h * 256 + 128], 4,
                                                   lambda k, hs=hs: rawv(cn, k, hs, 4), 128)
                                        so, bo = st16.next()
                                        eng = ev_engine()
                                        sc.op(eng, copy_op(eng, so[:], pb_[:, :]), reads=[bpb], writes=[bo])
                                        sc.dma(QBN[h, :, ts_], so[:], reads=[bo], writes=[dB["QBN"]])
                                        pa, ba = mp.next()
                                        fm_matmuls(pa, ba, lambda k, h=h: wuq_b[:, k, h * 256 + 128:h * 256 + 192], 4,
                                                   lambda k, hs=hs: rawv(cn, k, hs, 4), 64)
                                        pc, bc = mp.next()
                                        fm_matmuls(pc, bc, lambda k, h=h: wuq_b[:, k, h * 256 + 192:h * 256 + 256], 4,
                                                   lambda k, hs=hs: rawv(cn, k, hs, 4), 64)
                                        rope_combine(pa, ba, pc, bc, cos_ap, sin_ap, QBR[h, :, ts_], dB["QBR"])
                            else:
                                for h in range(6):
                                    for (hi, rhs_fn, t0, hs, _c, _s) in hv:
                                        ts_ = slice(t0, t0 + 512)
                                        pb_, bpb = mp.next()
                                        fm_matmuls(pb_, bpb, lambda k, h=h: wukv_b[:, k, h * 128:h * 128 + 128], 2,
                                                   lambda k, hs=hs: rawv(cn, k, hs, 2), 128)
                                        so, bo = st16.next()
                                        eng = ev_engine()
                                        sc.op(eng, copy_op(eng, so[:], pb_[:, :]), reads=[bpb], writes=[bo])
                                        sc.dma(KBN[h, :, ts_], so[:], reads=[bo], writes=[dB["KBN"]])
                                for tt in range(8):
                                    tsl = slice(tt * 128, tt * 128 + 128)
                                    for cg in range(2):
                                        pb_, bpb = mp.next()
                                        for k in range(2):
                                            sc.op("pe", lambda e, k=k, pb_=pb_, tsl=tsl, cg=cg: e.matmul(
                                                pb_[:, 0:384], lhsT=rawv(cn, k, tsl, 2),
                                                rhs=wukv_b[:, k, 768 + cg * 384:768 + cg * 384 + 384],
                                                start=(k == 0), stop=(k == 1)),
                                                reads=[bcn, bW], writes=[bpb], sig=(k == 1))
                                        so, bo = vst.next()
                                        eng = ev_engine()
                                        sc.op(eng, copy_op(eng, so[:, 0:384], pb_[:, 0:384]), reads=[bpb], writes=[bo])
                                        r0 = tok0 + tt * 128
                                        sc.dma(VB[r0:r0 + 128, cg * 384:cg * 384 + 384], so[:, 0:384], reads=[bo],
                                               writes=[dB["VB"]])
                    elif kind == "krope":
                        for (hi, rhs_fn, t0, hs, cos_ap, sin_ap) in halves("kv"):
                            ts_ = slice(t0, t0 + 512)
                            pa, ba = mp.next()
                            main_mm(pa, ba, 0, 64, rhs_fn)
                            pc, bc = mp.next()
                            main_mm(pc, bc, 64, 64, rhs_fn)
                            rope_combine(pa, ba, pc, bc, cos_ap, sin_ap, KBR[:, ts_], dB["KBR"])
                    if packed and gi + 2 < ngroups:
                        pre[gi % 2] = load_packed(gi + 2)
            sc.barrier()
        if stop_after == ("P", l):
            break

        with contextlib.ExitStack() as ps:
            def psb(name, shape, dt):
                return ps.enter_context(nc.sbuf_tensor(f"{name}_L{l}", list(shape), dt))

            def ppm(name, shape, dt=F32):
                return ps.enter_context(nc.psum_tensor(f"{name}_L{l}", list(shape), dt))

            QT = Ring([psb(f"QT{i}", [128, S], BF16) for i in range(2)])
            KT = Ring([psb(f"KT{i}", [128, S], BF16) for i in range(2)])
            VT = Ring([psb(f"VT{i}", [128, NT, 128], BF16) for i in range(2)])
            QR = Ring([psb(f"QR{i}", [64, S], BF16) for i in range(2)])
            KR = psb("KR", [64, S], BF16)
            bKR = Buf("KR")
            gch = Ring([psb(f"gch{i}", [128, 512], F32) for i in range(2)])
            PT = Ring([psb(f"PT{i}", [128, 512], BF16) for i in range(4)])
            tmpS = Ring([psb(f"tmpS{i}", [128, 128], F32) for i in range(6)])
            ep = Ring([psb(f"ep{i}", [128, 512], F32) for i in range(6)])
            mx16 = Ring([psb(f"mx{i}", [128, 512], BF16) for i in range(2)])
            sq16 = psb("sq16", [128, 512], BF16)
            bsq16 = Buf("sq16")
            nlf = psb("nlf", [6, S], F32)
            ncum = psb("ncum", [6, S], F32)
            ones6 = psb("ones6", [6, 1024], F32)
            ncumT = psb("ncumT", [128, NT, 6], F32)
            ncref = psb("ncref", [128, 6, NT], F32)
            biasA = Ring([psb(f"biasA{i}", [128, NT, NT], F32) for i in range(2)])
            biasC = Ring([psb(f"biasC{i}", [128, NT, 8], F32) for i in range(2)])
            fAr = Ring([psb(f"fA{i}", [128, NT], F32) for i in range(2)])
            lamt = psb("lamt", [128, 4, 64], F32)
            lamj = psb("lamj", [128, 64], F32)
            lamv = psb("lamv", [128, 4], F32)
            gsub = psb("gsub", [128, 1], F32)
            bL = Buf("lam")
            bcum = Buf("cum")
            NB3 = 384 if split else 256
            BT = psb("BT", [128, 4, NB3], F32)
            bBT = Buf("BT")
            mA2 = psb("mA2", [128, 256], F32)
            mB2 = psb("mB2", [128, 256], F32)
            if split:
                sc.dma(mA2[:], c_mA2[:, :], writes=[bBT])
                sc.dma(mB2[:], c_mB2[:, :], writes=[bBT])
            with nc.sbuf_tensor(f"bkt_L{l}", [128, 32, NB3], BF16) as bkt, \
                    nc.sbuf_tensor(f"addm_L{l}", [128, NB3], F32) as addm:
                bbk = Buf("bkt")
                sc.dma(bkt[:], (c_bkt3 if split else c_bkt)[:, :, :], writes=[bbk])
                sc.dma(addm[:], (c_add3 if split else c_add2)[:, :], writes=[bbk])
                for h in range(4):
                    for b in range(32):
                        col = rb_bc[:, b * 4 + h:b * 4 + h + 1]
                        if b == 0:
                            sc.op("dve", lambda e, h=h, b=b, col=col: e.tensor_scalar(
                                out=BT[:, h, :], in0=bkt[:, b, :], scalar1=col, scalar2=None, op0=ALU.mult),
                                reads=[bbk, bK], writes=[bBT])
                        else:
                            sc.op("dve", lambda e, h=h, b=b, col=col: e.scalar_tensor_tensor(
                                out=BT[:, h, :], in0=bkt[:, b, :], scalar=col, in1=BT[:, h, :], op0=ALU.mult,
                                op1=ALU.add), reads=[bbk, bK, bBT], writes=[bBT])
                    sc.op("dve", lambda e, h=h: e.tensor_tensor(
                        out=BT[:, h, :], in0=BT[:, h, :], in1=addm[:], op=ALU.add),
                        reads=[bBT, bbk], writes=[bBT])
                sc.barrier()

            Sb = Ring([ppm(f"Sb{i}", [128, 512]) for i in range(4)])
            OL = Ring([ppm(f"OL{i}", [128, 512]) for i in range(4)])

            for i, src in enumerate((lq1, lk1, lq2, lk2)):
                sc.dma(lamt[:, i, :], src[l:l + 1, :].partition_broadcast(128), writes=[bL])
            with nc.allow_non_contiguous_dma(reason="tiny param vectors"):
                sc.dma(gsub[:], subln_g[l, :].rearrange("(p o) -> p o", o=1), writes=[bL])
            sc.op("dve", lambda e: e.scalar_tensor_tensor(out=lamj[:], in0=lamt[:, 0, :], scalar=1.0, in1=lamt[:, 1, :],
                                                          op0=ALU.mult, op1=ALU.mult, accum_out=lamv[:, 0:1]),
                  reads=[bL], writes=[bL])
            sc.op("dve", lambda e: e.scalar_tensor_tensor(out=lamj[:], in0=lamt[:, 2, :], scalar=1.0, in1=lamt[:, 3, :],
                                                          op0=ALU.mult, op1=ALU.mult, accum_out=lamv[:, 1:2]),
                  reads=[bL], writes=[bL])
            sc.op("act", lambda e: e.activation(out=lamv[:, 0:2], in_=lamv[:, 0:2], func=AF.Exp), reads=[bL],
                  writes=[bL])
            sc.op("dve", lambda e: e.tensor_tensor(out=lamv[:, 2:3], in0=lamv[:, 1:2], in1=lamv[:, 0:1],
                                                   op=ALU.subtract), reads=[bL], writes=[bL])
            sc.op("dve", lambda e: e.tensor_scalar(out=lamv[:, 2:3], in0=lamv[:, 2:3], scalar1=-lam_init, scalar2=None,
                                                   op0=ALU.add), reads=[bL], writes=[bL])
            sc.op("dve", lambda e: e.tensor_scalar(out=lamv[:, 3:4], in0=gsub[:, 0:1], scalar1=1.0 - lam_init,
                                                   scalar2=None, op0=ALU.mult), reads=[bL], writes=[bL])
            neglam = lamv[:, 2:3]
            gsub_s = lamv[:, 3:4]

            sc.dma(nlf[:], NLF[:, :], reads=[dB["NLF"]], writes=[bcum])
            sc.op("pool", lambda e: e.memset(ones6[:], 1.0), writes=[bcum])
            for c in range(4):
                init = 0.0 if c == 0 else ncum[:, c * 1024 - 1:c * 1024]
                sc.op("dve", lambda e, c=c, init=init: e.tensor_tensor_scan(
                    out=ncum[:, c * 1024:(c + 1) * 1024], data0=ones6[:], data1=nlf[:, c * 1024:(c + 1) * 1024],
                    initial=init, op0=ALU.mult, op1=ALU.add), reads=[bcum], writes=[bcum])
            for kt in range(NT):
                pb_, bpb = Sb.next()
                sc.op("pe", lambda e, kt=kt, pb_=pb_: e.transpose(out=pb_[:, 0:6], in_=ncum[:, kt * 128:(kt + 1) * 128],
                                                                  identity=identf[0:6, 0:6]),
                      reads=[bcum, bK], writes=[bpb])
                sc.op("dve", lambda e, kt=kt, pb_=pb_: e.tensor_copy(out=ncumT[:, kt, :], in_=pb_[:, 0:6]),
                      reads=[bpb], writes=[bcum])
            for h in range(6):
                pb_, bpb = Sb.next()
                sc.op("pe", lambda e, h=h, pb_=pb_: e.matmul(
                    pb_[:, 0:NT], lhsT=sel[:, h * 128:(h + 1) * 128],
                    rhs=ncum[:, :].rearrange("p (a b) -> p a b", b=128)[:, :, 0], start=True, stop=True),
                    reads=[bcum, bK], writes=[bpb])
                sc.op("dve", lambda e, h=h, pb_=pb_: e.tensor_copy(out=ncref[:, h, :], in_=pb_[:, 0:NT]),
                      reads=[bpb], writes=[bcum])

            t_stop = stop_after[0] if (stop_after and stop_after[1] == l) else None
            nA = 0 if t_stop == "T0" else (int(os.environ.get("KNA", 6)))
            nB = 0 if t_stop in ("T0", "TA") else int(os.environ.get("KNB", 6))
            nC = 0 if t_stop in ("T0", "TA", "TB") else int(os.environ.get("KNC", 4))
            nJ = int(os.environ.get("KNJ", NOWN // 4))
            NQ = NOWN * 128

            def nkt_of(j):
                return (8 * j + 8) if split else (4 * j + 4)

            def c0_of(kt, j):
                i_min = (kt // 2) if split else kt
                return max(0, i_min - 4 * j) * 128

            def special(kt, i, with_prev):
                d = kt - (2 * i if split else i)
                hi = 1 if split else 0
                lo = -1 if with_prev else 0
                return (d + 1) if lo <= d <= hi else None

            if split:
                ncro = psb("ncro", [128, 6, NOWN], F32)
                nv = ncref[:].rearrange("p h (i two) -> p h i two", two=2)
                sc.op("dve", lambda e: e.tensor_scalar(out=ncro[:], in0=nv[:, :, :, 0], scalar1=ra, scalar2=None,
                                                       op0=ALU.mult), reads=[bcum, bK], writes=[bcum])
                sc.op("dve", lambda e: e.scalar_tensor_tensor(out=ncro[:], in0=nv[:, :, :, 1], scalar=rb_, in1=ncro[:],
                                                              op0=ALU.mult, op1=ALU.add),
                      reads=[bcum, bK], writes=[bcum])
            else:
                ncro = ncref
            mskA = (lambda idx: mA2[:, (idx - 1) * 128:(idx - 1) * 128 + 128]) if split else (lambda idx: maskA[:])
            mskB = (lambda idx: mB2[:, (idx - 1) * 128:(idx - 1) * 128 + 128]) if split else (lambda idx: maskB[:])

            def epilogue_AB(Ob, bO, Lb, bLb, gsrc, gname, h, j, ft):
                cs = slice(j * 512, j * 512 + 512)
                gt, bg = gch.next()
                sc.dma(gt[:], gsrc[h, :, cs], reads=[dB[gname]], writes=[bg])
                r1, b1 = ep.next()
                sc.op("dve", lambda e: e.reciprocal(out=r1[:], in_=Lb[:, 0:512]), reads=[bLb], writes=[b1])
                r2, b2 = ep.next()
                sc.op("dve", lambda e: e.tensor_tensor(out=r2[:], in0=Ob[:, 0:512], in1=r1[:], op=ALU.mult),
                      reads=[bO, b1], writes=[b2])
                mo, bm = mx16.next()
                sc.op("pool", lambda e: e.tensor_tensor(out=mo[:], in0=r2[:], in1=gt[:], op=ALU.mult),
                      reads=[b2, bg], writes=[bm])
                sc.dma(MIX[ft, :, cs], mo[:], reads=[bm], writes=[dB["MIX"]])

            scaleA = 1.0 / math.sqrt(128.0)

            def load_head_A(h):
                q, bq = QT.next()
                k, bk = KT.next()
                v, bv = VT.next()
                sc.dma(q[:, 0:NQ], QA[h, :, 0:NQ], reads=[dB["QA"]], writes=[bq])
                sc.dma(k[:], KA[h, :, :], reads=[dB["KA"]], writes=[bk])
                sc.dma(v[:], VA[:, h * 128:(h + 1) * 128].rearrange("(t p) d -> p t d", p=128), reads=[dB["VA"]],
                       writes=[bv])
                return (q, bq, k, bk, v, bv)

            def prep_A(h):
                bA, bbA = biasA.next()
                bC, bbC = biasC.next()
                fA, bfA = fAr.next()
                NJ_ = NOWN // 4
                for kt in range(NT):
                    sc.op("dve", lambda e, kt=kt, h=h, bA=bA: e.tensor_scalar(
                        out=bA[:, kt, 0:NOWN], in0=ncro[:, h, :], scalar1=-1.0, scalar2=ncumT[:, kt, h:h + 1],
                        op0=ALU.mult, op1=ALU.add), reads=[bcum], writes=[bbA])
                    sc.op("dve", lambda e, kt=kt, h=h, bC=bC: e.tensor_scalar(
                        out=bC[:, kt, 0:NJ_], in0=ncro[:, h, :].rearrange("p (j f) -> p j f", f=4)[:, :, 0],
                        scalar1=-1.0, scalar2=ncumT[:, kt, h:h + 1], op0=ALU.mult, op1=ALU.add),
                        reads=[bcum], writes=[bbC])
                for j in range(NJ_):
                    sc.op("dve", lambda e, j=j, h=h, fA=fA: e.tensor_scalar(
                        out=fA[:, 4 * j:4 * j + 4], in0=ncro[:, h, 4 * j:4 * j + 4], scalar1=-1.0,
                        scalar2=ncro[:, h, 4 * j:4 * j + 1], op0=ALU.mult, op1=ALU.add), reads=[bcum], writes=[bfA])
                sc.op("act", lambda e, fA=fA: e.activation(out=fA[:, 0:NOWN], in_=fA[:, 0:NOWN], func=AF.Exp),
                      reads=[bfA], writes=[bfA])
                return (bA, bbA, bC, bbC, fA, bfA)

            prepA = prep_A(0) if nA > 0 else None
            nxt = load_head_A(0)
            for h in range(nA):
                q, bq, k, bk, v, bv = nxt
                if h + 1 < nA:
                    nxt = load_head_A(h + 1)
                bA, bbA, bC, bbC, fA, bfA = prepA
                for j in range(nJ):
                    if j == max(nJ - 1, 0) and h + 1 < nA:
                        prepA_next = prep_A(h + 1)
                    Oo, bOo = OL.next()
                    Lo, bLo = OL.next()
                    Od, bOd = OL.next()
                    Ld, bLd = OL.next()
                    nkt = nkt_of(j)
                    n_off = (8 * j) if split else (4 * j)
                    pend = None
                    for kt in range(nkt + 1):
                        if kt < nkt:
                            off = kt < n_off
                            c0 = 0 if off else c0_of(kt, j)
                            Sp, bS = Sb.next()
                            sc.op("pe", lambda e, kt=kt, c0=c0, Sp=Sp: e.matmul(
                                Sp[:, c0:512], lhsT=k[:, kt * 128:(kt + 1) * 128],
                                rhs=q[:, j * 512 + c0:j * 512 + 512], start=True, stop=True),
                                reads=[bk, bq], writes=[bS])
                            pt, bpt = PT.next()
                            if off:
                                sc.op("act", lambda e, Sp=Sp, pt=pt, kt=kt, j=j: e.activation(
                                    out=pt[:, 0:512], in_=Sp[:, 0:512], func=AF.Exp, bias=bC[:, kt, j:j + 1],
                                    scale=scaleA), reads=[bS, bbC], writes=[bpt])
                            else:
                                for t in range(c0 // 128, 4):
                                    qb = 4 * j + t
                                    cs = slice(t * 128, t * 128 + 128)
                                    bias = bA[:, kt, qb:qb + 1]
                                    sidx = special(kt, qb, False)
                                    if sidx is not None:
                                        tm, btm = tmpS.next()
                                        sc.op("dve", lambda e, tm=tm, Sp=Sp, cs=cs, sidx=sidx: e.scalar_tensor_tensor(
                                            out=tm[:], in0=Sp[:, cs], scalar=scaleA, in1=mskA(sidx), op0=ALU.mult,
                                            op1=ALU.add), reads=[bK, bBT], writes=[btm, bS])
                                        sc.op("act", lambda e, tm=tm, pt=pt, cs=cs, bias=bias: e.activation(
                                            out=pt[:, cs], in_=tm[:], func=AF.Exp, bias=bias, scale=1.0),
                                            reads=[btm, bbA], writes=[bpt])
                                    else:
                                        sc.op("act", lambda e, Sp=Sp, pt=pt, cs=cs, bias=bias: e.activation(
                                            out=pt[:, cs], in_=Sp[:, cs], func=AF.Exp, bias=bias, scale=scaleA),
                                            reads=[bS, bbA], writes=[bpt])
                            cur = (kt, c0, pt, bpt, off)
                        else:
                            cur = None
                        if pend is not None:
                            pk, pc0, ppt, pbpt, poff = pend
                            Ot, bOt, Lt, bLt = (Oo, bOo, Lo, bLo) if poff else (Od, bOd, Ld, bLd)
                            st_ = (pk == 0) if poff else (pk == n_off)
                            sp_ = (pk == n_off - 1) if poff else (pk == nkt - 1)
                            sc.op("pe", lambda e, pk=pk, pc0=pc0, ppt=ppt, Ot=Ot, st_=st_, sp_=sp_: e.matmul(
                                Ot[:, pc0:512], lhsT=v[:, pk, :], rhs=ppt[:, pc0:512], start=st_, stop=sp_),
                                reads=[bv, pbpt], writes=[bOt], sig=False)
                            sc.op("pe", lambda e, pk=pk, pc0=pc0, ppt=ppt, Lt=Lt, st_=st_, sp_=sp_: e.matmul(
                                Lt[:, pc0:512], lhsT=ones[:], rhs=ppt[:, pc0:512], start=st_, stop=sp_),
                                reads=[bK, pbpt], writes=[bLt])
                        pend = cur
                    if n_off == 0:
                        epilogue_AB(Od, bOd, Ld, bLd, GA, "GA", h, j, h)
                    else:
                        to, bto = ep.next()
                        tl, btl = ep.next()
                        for t in range(4):
                            i = 4 * j + t
                            cs = slice(t * 128, t * 128 + 128)
                            sc.op("dve", lambda e, cs=cs, i=i, to=to: e.tensor_scalar(
                                out=to[:, cs], in0=Oo[:, cs], scalar1=fA[:, i:i + 1], scalar2=None, op0=ALU.mult),
                                reads=[bOo, bfA], writes=[bto])
                            sc.op("dve", lambda e, cs=cs, i=i, tl=tl: e.tensor_scalar(
                                out=tl[:, cs], in0=Lo[:, cs], scalar1=fA[:, i:i + 1], scalar2=None, op0=ALU.mult),
                                reads=[bLo, bfA], writes=[btl])
                        sc.op("dve", lambda e, to=to: e.tensor_tensor(out=to[:], in0=Od[:, :], in1=to[:], op=ALU.add),
                              reads=[bOd, bto], writes=[bto])
                        sc.op("dve", lambda e, tl=tl: e.tensor_tensor(out=tl[:], in0=Ld[:, :], in1=tl[:], op=ALU.add),
                              reads=[bLd, btl], writes=[btl])
                        epilogue_AB(to, bto, tl, btl, GA, "GA", h, j, h)
                if h + 1 < nA:
                    prepA = prepA_next

            scaleB = 1.0 / math.sqrt(192.0)
            sc.dma(KR[:], KBR[:, :], reads=[dB["KBR"]], writes=[bKR])

            def load_head_B(h):
                q, bq = QT.next()
                k, bk = KT.next()
                v, bv = VT.next()
                qr, bqr = QR.next()
                sc.dma(q[:, 0:NQ], QBN[h, :, 0:NQ], reads=[dB["QBN"]], writes=[bq])
                sc.dma(k[:], KBN[h, :, :], reads=[dB["KBN"]], writes=[bk])
                sc.dma(qr[:, 0:NQ], QBR[h, :, 0:NQ], reads=[dB["QBR"]], writes=[bqr])
                sc.dma(v[:], VB[:, h * 128:(h + 1) * 128].rearrange("(t p) d -> p t d", p=128), reads=[dB["VB"]],
                       writes=[bv])
                return (q, bq, k, bk, v, bv, qr, bqr)

            nxt = load_head_B(0)
            for h in range(nB):
                q, bq, k, bk, v, bv, qr, bqr = nxt
                if h + 1 < nB:
                    nxt = load_head_B(h + 1)
                for j in range(nJ):
                    Ob, bO = OL.next()
                    Lb, bLb = OL.next()
                    nkt = nkt_of(j)
                    pend = None
                    for kt in range(nkt + 1):
                        if kt < nkt:
                            c0 = c0_of(kt, j)
                            Sp, bS = Sb.next()
                            sc.op("pe", lambda e, kt=kt, c0=c0, Sp=Sp: e.matmul(
                                Sp[:, c0:512], lhsT=k[:, kt * 128:(kt + 1) * 128],
                                rhs=q[:, j * 512 + c0:j * 512 + 512], start=True, stop=False),
                                reads=[bk, bq], writes=[bS], sig=False)
                            sc.op("pe", lambda e, kt=kt, c0=c0, Sp=Sp: e.matmul(
                                Sp[:, c0:512], lhsT=KR[:, kt * 128:(kt + 1) * 128],
                                rhs=qr[:, j * 512 + c0:j * 512 + 512], start=False, stop=True),
                                reads=[bKR, bqr], writes=[bS])
                            pt, bpt = PT.next()
                            c1 = c0
                            sidx = special(kt, 4 * j + c0 // 128, False)
                            if sidx is not None:
                                cs = slice(c0, c0 + 128)
                                tm, btm = tmpS.next()
                                sc.op("dve", lambda e, tm=tm, Sp=Sp, cs=cs, sidx=sidx: e.scalar_tensor_tensor(
                                    out=tm[:], in0=Sp[:, cs], scalar=scaleB, in1=mskB(sidx), op0=ALU.mult,
                                    op1=ALU.add), reads=[bK, bBT], writes=[btm, bS])
                                sc.op("act", lambda e, tm=tm, pt=pt, cs=cs: e.activation(
                                    out=pt[:, cs], in_=tm[:], func=AF.Exp), reads=[btm], writes=[bpt])
                                c1 = c0 + 128
                            if c1 < 512:
                                sc.op("act", lambda e, Sp=Sp, pt=pt, c1=c1: e.activation(
                                    out=pt[:, c1:512], in_=Sp[:, c1:512], func=AF.Exp, scale=scaleB),
                                    reads=[bS], writes=[bpt])
                            cur = (kt, c0, pt, bpt)
                        else:
                            cur = None
                        if pend is not None:
                            pk, pc0, ppt, pbpt = pend
                            sc.op("pe", lambda e, pk=pk, pc0=pc0, ppt=ppt: e.matmul(
                                Ob[:, pc0:512], lhsT=v[:, pk, :], rhs=ppt[:, pc0:512], start=(pk == 0),
                                stop=(pk == nkt - 1)), reads=[bv, pbpt], writes=[bO], sig=False)
                            sc.op("pe", lambda e, pk=pk, pc0=pc0, ppt=ppt: e.matmul(
                                Lb[:, pc0:512], lhsT=ones[:], rhs=ppt[:, pc0:512], start=(pk == 0),
                                stop=(pk == nkt - 1)), reads=[bK, pbpt], writes=[bLb])
                        pend = cur
                    epilogue_AB(Ob, bO, Lb, bLb, GB, "GB", h, j, 6 + h)

            scaleC = 1.0 / 8.0

            def load_head_C(h):
                q, bq = QT.next()
                k, bk = KT.next()
                v, bv = VT.next()
                sc.dma(q[:, 0:NQ], QC[h, :, 0:NQ], reads=[dB["QC"]], writes=[bq])
                sc.dma(k[:], KC[h, :, :], reads=[dB["KC"]], writes=[bk])
                sc.dma(v[:], VC[:, h * 128:(h + 1) * 128].rearrange("(t p) d -> p t d", p=128), reads=[dB["VC"]],
                       writes=[bv])
                return (q, bq, k, bk, v, bv)

            pendC = []

            def flushC():
                while pendC:
                    pendC.pop(0)()

            nxt = load_head_C(0)
            for h in range(nC):
                q, bq, k, bk, v, bv = nxt
                if h + 1 < nC:
                    nxt = load_head_C(h + 1)
                b15 = rb_bc[:, 60 + h:61 + h]
                for j in range(nJ):
                    Os = [OL.next() for _ in range(4)]
                    nkt = nkt_of(j)
                    pend = None
                    for kt in range(nkt + 1):
                        if kt < nkt:
                            c0 = c0_of(kt, j)
                            if kt == min(3, nkt - 1):
                                flushC()
                            pts = []
                            for m in range(2):
                                Sp, bS = Sb.next()
                                ps_ = slice(m * 64, m * 64 + 64)
                                sc.op("pe", lambda e, kt=kt, c0=c0, Sp=Sp, ps_=ps_: e.matmul(
                                    Sp[:, c0:512], lhsT=k[ps_, kt * 128:(kt + 1) * 128],
                                    rhs=q[ps_, j * 512 + c0:j * 512 + 512], start=True, stop=True),
                                    reads=[bk, bq], writes=[bS])
                                pt, bpt = PT.next()
                                cfar = 512
                                for t in range(c0 // 128, 4):
                                    qb = 4 * j + t
                                    cs = slice(t * 128, t * 128 + 128)
                                    off = special(kt, qb, True)
                                    if off is not None:
                                        tm, btm = tmpS.next()
                                        sc.op("dve", lambda e, tm=tm, Sp=Sp, cs=cs, off=off: e.scalar_tensor_tensor(
                                            out=tm[:], in0=Sp[:, cs], scalar=scaleC,
                                            in1=BT[:, h, off * 128:off * 128 + 128], op0=ALU.mult, op1=ALU.add),
                                            reads=[bBT], writes=[btm, bS])
                                        sc.op("act", lambda e, tm=tm, pt=pt, cs=cs: e.activation(
                                            out=pt[:, cs], in_=tm[:], func=AF.Exp), reads=[btm], writes=[bpt])
                                    else:
                                        cfar = min(cfar, t * 128)
                                if cfar < 512:
                                    sc.op("act", lambda e, Sp=Sp, pt=pt, cfar=cfar: e.activation(
                                        out=pt[:, cfar:512], in_=Sp[:, cfar:512], func=AF.Exp, bias=b15,
                                        scale=scaleC), reads=[bS, bK], writes=[bpt])
                                pts.append((pt, bpt))
                            cur = (kt, c0, pts)
                        else:
                            cur = None
                        if pend is not None:
                            pk, pc0, ppts = pend
                            for m in range(2):
                                ppt, pbpt = ppts[m]
                                Ob, bO = Os[2 * m]
                                Lb, bLb = Os[2 * m + 1]
                                sc.op("pe", lambda e, pk=pk, pc0=pc0, ppt=ppt, Ob=Ob: e.matmul(
                                    Ob[:, pc0:512], lhsT=v[:, pk, :], rhs=ppt[:, pc0:512], start=(pk == 0),
                                    stop=(pk == nkt - 1)), reads=[bv, pbpt], writes=[bO], sig=False)
                                sc.op("pe", lambda e, pk=pk, pc0=pc0, ppt=ppt, Lb=Lb: e.matmul(
                                    Lb[:, pc0:512], lhsT=ones[:], rhs=ppt[:, pc0:512], start=(pk == 0),
                                    stop=(pk == nkt - 1)), reads=[bK, pbpt], writes=[bLb])
                        pend = cur
                    cs = slice(j * 512, j * 512 + 512)
                    gt, bg = gch.next()
                    sc.dma(gt[:], GC[h, :, cs], reads=[dB["GC"]], writes=[bg])
                    (O1, bO1), (L1, bL1), (O2, bO2), (L2, bL2) = Os
                    r1, b1 = ep.next()
                    a1, ba1 = ep.next()
                    r2, b2 = ep.next()
                    a2, ba2 = ep.next()
                    sc.op("dve", lambda e: e.tensor_copy(out=r1[:], in_=L1[:, :]), reads=[bL1], writes=[b1])
                    sc.op("dve", lambda e: e.tensor_copy(out=a1[:], in_=O1[:, :]), reads=[bO1], writes=[ba1])
                    sc.op("dve", lambda e: e.tensor_copy(out=r2[:], in_=L2[:, :]), reads=[bL2], writes=[b2])
                    sc.op("dve", lambda e: e.tensor_copy(out=a2[:], in_=O2[:, :]), reads=[bO2], writes=[ba2])
                    sc.op("dve", lambda e: e.reciprocal(out=r1[:], in_=r1[:]), reads=[b1], writes=[b1])
                    sc.op("dve", lambda e: e.tensor_tensor(out=a1[:], in0=a1[:], in1=r1[:], op=ALU.mult),
                          reads=[ba1, b1], writes=[ba1])
                    sc.op("dve", lambda e: e.reciprocal(out=r2[:], in_=r2[:]), reads=[b2], writes=[b2])
                    sc.op("dve", lambda e: e.tensor_tensor(out=a2[:], in0=a2[:], in1=r2[:], op=ALU.mult),
                          reads=[ba2, b2], writes=[ba2])
                    o, bo = ep.next()
                    sc.op("dve", lambda e: e.scalar_tensor_tensor(out=o[:], in0=a2[:], scalar=neglam, in1=a1[:],
                                                                  op0=ALU.mult, op1=ALU.add),
                          reads=[ba1, ba2, bL], writes=[bo])
                    sc.op("act", lambda e: e.activation(out=sq16[:], in_=o[:], func=AF.Square), reads=[bo],
                          writes=[bsq16])
                    def part2(o=o, bo=bo, r1=r1, b1=b1, a1=a1, ba1=ba1, gt=gt, bg=bg, cs=cs, h=h):
                        Sp, bS = Sb.next()
                        sc.op("pe", lambda e: e.matmul(Sp[:, :], lhsT=ones[:], rhs=sq16[:], start=True, stop=True),
                              reads=[bsq16, bK], writes=[bS])
                        rt, brt = ep.next()
                        sc.op("act", lambda e: e.activation(out=rt[:], in_=Sp[:, :], func=AF.Sqrt, scale=1.0 / 128.0,
                                                            bias=EPS), reads=[bS], writes=[brt])
                        sc.op("dve", lambda e: e.reciprocal(out=r1[:], in_=rt[:]), reads=[brt], writes=[b1])
                        sc.op("dve", lambda e: e.scalar_tensor_tensor(out=a1[:], in0=o[:], scalar=gsub_s, in1=r1[:],
                                                                      op0=ALU.mult, op1=ALU.mult),
                              reads=[bo, b1, bL], writes=[ba1])
                        mo, bm = mx16.next()
                        sc.op("pool", lambda e: e.tensor_tensor(out=mo[:], in0=a1[:], in1=gt[:], op=ALU.mult),
                              reads=[ba1, bg], writes=[bm])
                        sc.dma(MIX[12 + h, :, cs], mo[:], reads=[bm], writes=[dB["MIX"]])
                    pendC.append(part2)
            flushC()
            sc.barrier()
        if stop_after in (("T", l), ("T0", l), ("TA", l), ("TB", l)):
            break

        with contextlib.ExitStack() as ps:
            def psb(name, shape, dt):
                return ps.enter_context(nc.sbuf_tensor(f"{name}_L{l}", list(shape), dt))

            def ppm(name, shape, dt=F32):
                return ps.enter_context(nc.psum_tensor(f"{name}_L{l}", list(shape), dt))

            wo = psb("wo", [128, 16, D], BF16)
            bwo = [Buf(f"wo{i}") for i in range(8)]
            bwo2 = [Buf(f"wo2_{i}") for i in range(8)]
            wos = Ring([psb(f"wos{i}", [128, 16, 256], F32) for i in range(2)])
            mxr = Ring([psb(f"mxt{i}", [128, 16, 512], BF16) for i in range(2)])
            xr = Ring([psb(f"xr{i}", [128, D], F32) for i in range(2)])
            xn = Ring([psb(f"xn{i}", [128, D], F32) for i in range(2)])
            yo = Ring([psb(f"yo{i}", [128, D], F32) for i in range(2)])
            fg = psb("fg", [128, D], F32)
            bfg = Buf("fg")
            junk = psb("junk", [128, D], BF16)
            bjunk = Buf("junk")
            ssq = Ring([psb(f"ossq{i}", [128, 1], F32) for i in range(2)])
            rstd = Ring([psb(f"orstd{i}", [128, 1], F32) for i in range(2)])
            op_ = Ring([ppm(f"op{i}", [128, 512]) for i in range(4)])
            if last:
                sc.dma(fg[:], final_g.rearrange("(o d) -> o d", o=1).partition_broadcast(128), writes=[bfg])
            wo_l = w_out[l]
            pre = []
            for cg in range(2):
                t, b = wos.next()
                sc.dma(t[:], wo_l[:, cg * 256:cg * 256 + 256].rearrange("(k p) n -> p k n", p=128), writes=[b])
                pre.append((t, b))
            for cg in range(8):
                t, b = pre[cg % 2]
                sc.op("act", lambda e, t=t, cg=cg: e.copy(out=wo[:, 0:8, cg * 256:cg * 256 + 256], in_=t[:, 0:8, :]),
                      reads=[b], writes=[bwo[cg]])
                sc.op("dve", lambda e, t=t, cg=cg: e.tensor_copy(out=wo[:, 8:16, cg * 256:cg * 256 + 256],
                                                                 in_=t[:, 8:16, :]),
                      reads=[b], writes=[bwo2[cg]])
                if cg + 2 < 8:
                    t2, b2 = wos.next()
                    sc.dma(t2[:], wo_l[:, (cg + 2) * 256:(cg + 2) * 256 + 256].rearrange("(k p) n -> p k n", p=128),
                           writes=[b2])
                    pre[cg % 2] = (t2, b2)
            ydst = y_out if last else X1
            ydname = "Y" if last else "X1"
            if split:
                xsel = psb("xsel", [128, D], F32)
                bxsel = Buf("xsel")
            for c4 in range(NOWN // 4):
                mt, bmt = mxr.next()
                sc.dma(mt[:], MIX[:, :, c4 * 512:c4 * 512 + 512].rearrange("f p t -> p f t"), reads=[dB["MIX"]],
                       writes=[bmt])
                for t4 in range(4):
                    tt = c4 * 4 + t4
                    if split:
                        xa, bxa = xr.next()
                        xb, bxb = xr.next()
                        sc.dma(xa[:], xsrc[2 * tt * 128:2 * tt * 128 + 128, :], reads=[xsrc_b], writes=[bxa])
                        sc.dma(xb[:], xsrc[(2 * tt + 1) * 128:(2 * tt + 1) * 128 + 128, :], reads=[xsrc_b],
                               writes=[bxb])
                        sc.op("act", lambda e, xa=xa: e.activation(out=xsel[:], in_=xa[:], func=AF.Copy, scale=ra),
                              reads=[bxa, bK], writes=[bxsel])
                        sc.op("dve", lambda e, xb=xb: e.scalar_tensor_tensor(out=xsel[:], in0=xb[:], scalar=rb_,
                                                                             in1=xsel[:], op0=ALU.mult, op1=ALU.add),
                              reads=[bxb, bxsel, bK], writes=[bxsel])
                        xt, bx = xsel, bxsel
                    else:
                        xt, bx = xr.next()
                        sc.dma(xt[:], xsrc[tt * 128:tt * 128 + 128, :], reads=[xsrc_b], writes=[bx])
                    xo, bxo = xn.next()
                    for cg in range(4):
                        pb_, bpb = op_.next()
                        for m in range(16):
                            sc.op("pe", lambda e, m=m, pb_=pb_, cg=cg, t4=t4, mt=mt: e.matmul(
                                pb_[:, :], lhsT=mt[:, m, t4 * 128:t4 * 128 + 128],
                                rhs=wo[:, m, cg * 512:cg * 512 + 512], start=(m == 0), stop=(m == 15)),
                                reads=[bmt, bwo[2 * cg], bwo[2 * cg + 1], bwo2[2 * cg], bwo2[2 * cg + 1]],
                                writes=[bpb], sig=(m == 15))
                        sc.op("dve", lambda e, pb_=pb_, cg=cg, xo=xo, xt=xt: e.tensor_tensor(
                            out=xo[:, cg * 512:cg * 512 + 512], in0=pb_[:, :], in1=xt[:, cg * 512:cg * 512 + 512],
                            op=ALU.add), reads=[bpb, bx], writes=[bxo])
                    if not last:
                        sc.dma(X1[tt * 128:tt * 128 + 128, :], xo[:], reads=[bxo], writes=[dB["X1"]])
                    else:
                        sq_t, bsq = ssq.next()
                        rs_t, brs = rstd.next()
                        sc.op("act", lambda e, xo=xo, sq_t=sq_t: e.activation(out=junk[:], in_=xo[:], func=AF.Square,
                                                                               accum_out=sq_t[:]),
                              reads=[bxo], writes=[bjunk, bsq])
                        sc.op("act", lambda e, sq_t=sq_t: e.activation(out=sq_t[:], in_=sq_t[:], func=AF.Sqrt,
                                                                       scale=1.0 / D, bias=EPS),
                              reads=[bsq], writes=[bsq])
                        sc.op("dve", lambda e, sq_t=sq_t, rs_t=rs_t: e.reciprocal(out=rs_t[:], in_=sq_t[:]),
                              reads=[bsq], writes=[brs])
                        yt, byt = yo.next()
                        sc.op("dve", lambda e, xo=xo, rs_t=rs_t, yt=yt: e.scalar_tensor_tensor(
                            out=yt[:], in0=xo[:], scalar=rs_t[:, 0:1], in1=fg[:], op0=ALU.mult, op1=ALU.mult),
                            reads=[bxo, brs, bfg], writes=[byt])
                        sc.dma(y_out[tt * 128:tt * 128 + 128, :], yt[:], reads=[byt], writes=[dB["Y"]])
            sc.barrier()
        if stop_after == ("O", l):
            break

    sc.finish()
    stack.close()
    return nc, sc


def _t5_bucket(rel):
    nb = 16
    me = 8
    ret = (rel > 0).astype(np.int32) * nb
    n = np.abs(rel)
    nf = np.maximum(n, 1).astype(np.float32)
    large = me + (np.log(nf / me) / math.log(128 / me) * (nb - me)).astype(np.int32)
    large = np.minimum(large, nb - 1)
    return ret + np.where(n < me, n, large)


def host_constants(r=0):
    c = {}
    c["c_ident"] = np.eye(128, dtype=np.float32).astype(ml_dtypes.bfloat16)
    c["c_identf"] = np.eye(128, dtype=np.float32)
    c["c_ones"] = np.ones((128, 128), np.float32).astype(ml_dtypes.bfloat16)
    half = 32
    inv = (10000.0 ** (-np.arange(half, dtype=np.float32) / half)).astype(np.float32)
    ang = (np.arange(S, dtype=np.float32)[:, None] * inv[None, :]).astype(np.float32)
    cos = np.cos(ang).T.astype(np.float32)
    sin = np.sin(ang).T.astype(np.float32)
    cos2 = np.ascontiguousarray(np.concatenate([cos, cos], axis=0))
    sin2 = np.ascontiguousarray(np.concatenate([-sin, sin], axis=0))
    c["c_cos2"] = cos2
    c["c_sin2"] = sin2
    own_cols = np.concatenate([np.arange((2 * i + r) * 128, (2 * i + r + 1) * 128) for i in range(NT // 2)])
    c["c_cos_own"] = np.ascontiguousarray(cos2[:, own_cols])
    c["c_sin_own"] = np.ascontiguousarray(sin2[:, own_cols])
    kl = np.arange(128)[:, None]
    ql = np.arange(128)[None, :]
    mA = np.where(kl <= ql, 0.0, NEG).astype(np.float32)
    mB = np.where((kl // 64) <= (ql // 64), 0.0, NEG).astype(np.float32)
    c["c_maskA"] = mA
    c["c_maskB"] = mB
    closed = np.full((128, 128), NEG, np.float32)
    opened = np.zeros((128, 128), np.float32)
    c["c_mA2"] = np.ascontiguousarray(np.concatenate([mA, closed] if r == 0 else [opened, mA], axis=1))
    c["c_mB2"] = np.ascontiguousarray(np.concatenate([mB, closed] if r == 0 else [opened, mB], axis=1))
    bk = np.zeros((128, 32, 256), np.float32)
    for off, d in ((0, -128), (1, 0)):
        bu = _t5_bucket(kl - ql + d)
        for b in range(32):
            bk[:, b, off * 128:(off + 1) * 128] = (bu == b)
    c["c_bkt"] = bk.astype(ml_dtypes.bfloat16)
    c["c_add2"] = np.ascontiguousarray(np.concatenate([opened, mB], axis=1))
    bk3 = np.zeros((128, 32, 384), np.float32)
    add3 = np.zeros((128, 384), np.float32)
    for idx in range(3):
        delta = idx - 1 - r
        sl = slice(idx * 128, (idx + 1) * 128)
        if delta > 0:
            add3[:, sl] = NEG
            continue
        bu = _t5_bucket(kl - ql + delta * 128)
        for b in range(32):
            bk3[:, b, sl] = (bu == b)
        if delta == 0:
            add3[:, sl] = mB
    c["c_bkt3"] = bk3.astype(ml_dtypes.bfloat16)
    c["c_add3"] = add3
    sel = np.zeros((6, 768), np.float32)
    for h in range(6):
        sel[h, h * 128:(h + 1) * 128] = 1.0
    c["c_sel"] = sel
    role = np.zeros((128, 2), np.float32)
    role[:, r] = 1.0
    c["c_role"] = role
    return c


_CACHE = {}


def kernel(x, norm_g, w_in, b_forget, mla_q_norm_g, w_uq, mla_kv_norm_g, w_ukv,
           lambda_q1, lambda_k1, lambda_q2, lambda_k2, diff_subln_g, rel_bias,
           w_out, final_norm_g):
    f = lambda a: np.ascontiguousarray(np.asarray(a, dtype=np.float32))
    if "nc" not in _CACHE:
        _CACHE["nc"] = build_program()[0]
    nc = _CACHE["nc"]
    consts = [host_constants(0), host_constants(1)]
    shared = dict(norm_g=f(norm_g), w_in=f(w_in), b_forget=f(b_forget), mla_q_norm_g=f(mla_q_norm_g),
                  w_uq=f(w_uq), mla_kv_norm_g=f(mla_kv_norm_g), w_ukv=f(w_ukv), lambda_q1=f(lambda_q1),
                  lambda_k1=f(lambda_k1), lambda_q2=f(lambda_q2), lambda_k2=f(lambda_k2),
                  diff_subln_g=f(diff_subln_g), rel_bias=f(rel_bias), w_out=f(w_out),
                  final_norm_g=f(final_norm_g))
    xs = f(x)
    in_maps = []
    for c in range(8):
        m = dict(shared)
        m.update(consts[c % 2])
        m["x"] = xs[c // 2]
        in_maps.append(m)
    res = run_bass_kernel_spmd(nc, in_maps, core_ids=list(range(8)))
    out = np.empty((4, S, D), np.float32)
    for c in range(8):
        b, r = c // 2, c % 2
        y = np.asarray(res.results[c]["y"], dtype=np.float32).reshape(NT // 2, 128, D)
        out.reshape(4, NT // 2, 2, 128, D)[b, :, r] = y
    return out
```

```python
import contextlib
import math
import os

import numpy as np
import ml_dtypes

import concourse.bass as bass
import concourse.mybir as mybir
from concourse.bass_utils import run_bass_kernel_spmd

F32 = mybir.dt.float32
BF16 = mybir.dt.bfloat16
AF = mybir.ActivationFunctionType
ALU = mybir.AluOpType
AX = mybir.AxisListType

D = 2048
S = 4096
DEPTH = 2
NIN = 6726
EPS = 1e-6
NEG = -30000.0
KSKIP = os.environ.get('KSKIP', '').split(',')
SILU_FUNC = AF.Copy if os.environ.get('KDBG_SILU') == 'copy' else AF.Silu
NT = S // 128
TS = 1024
O_AQ, O_AK, O_AV, O_AF, O_AG = 0, 768, 1536, 2304, 2310
O_BCQ, O_BCKV, O_BKR, O_BG = 3078, 3590, 3846, 3910
O_CQ, O_CK, O_CV, O_CG = 4678, 5190, 5702, 6214


class Buf:
    __slots__ = ("name", "w", "rs", "rd")

    def __init__(self, name=""):
        self.name = name
        self.w = None
        self.rs = {}
        self.rd = []


class Sched:
    CE = ("pe", "act", "dve", "pool")

    def __init__(self, nc, stack):
        self.nc = nc
        self.E = {"pe": nc.tensor, "act": nc.scalar, "dve": nc.vector, "pool": nc.gpsimd, "sp": nc.sync}
        self.sem = {k: stack.enter_context(nc.semaphore("s_" + k)) for k in self.CE}
        self.cnt = {k: 0 for k in self.CE}
        self.pending = {k: [] for k in self.CE}
        self.waited = {k: {} for k in self.E}
        self.KD = 8
        self.dq = ("sp", "act")
        self.dsem = {q: [stack.enter_context(nc.semaphore(f"d_{q}{i}")) for i in range(self.KD)] for q in self.dq}
        self.dcnt = {q: 0 for q in self.dq}
        self.dtoks = {q: [None] * self.KD for q in self.dq}
        self.last = {}
        self.nwaits = 0

    def _wait(self, e, tok):
        if tok is None:
            return
        key, sem, val, _ = tok
        assert val is not None, "dependency on an unsignaled instruction"
        w = self.waited[e]
        if w.get(key, 0) >= val:
            return
        self.E[e].wait_ge(sem, val)
        self.nwaits += 1
        w[key] = val

    def _deps(self, e, reads, writes):
        for b in reads:
            if b.w is not None:
                self._wait(e, b.w)
        for b in writes:
            if b.w is not None and not (b.w[3] == e and e == "pe"):
                self._wait(e, b.w)
            for en, t in b.rs.items():
                if not (en == e and e == "pe"):
                    self._wait(e, t)
            for t in b.rd:
                self._wait(e, t)

    def _record(self, e, tok, reads, writes, is_dma):
        for b in reads:
            if is_dma:
                b.rd.append(tok)
                if len(b.rd) > 64:
                    b.rd = b.rd[-64:]
            else:
                b.rs[e] = tok
        for b in writes:
            b.w = tok
            b.rs = {}
            b.rd = []

    def op(self, e, fn, reads=(), writes=(), sig=True):
        self._deps(e, reads, writes)
        ins = fn(self.E[e])
        tok = [e, self.sem[e], None, e]
        if sig:
            self.cnt[e] += 1
            ins.then_inc(self.sem[e], 1)
            tok[2] = self.cnt[e]
            for p in self.pending[e]:
                p[2] = self.cnt[e]
            self.pending[e] = []
        else:
            self.pending[e].append(tok)
        self._record(e, tok, reads, writes, False)
        self.last[e] = tok
        return tok

    def dma(self, out, in_, reads=(), writes=(), q="sp", **kw):
        for b in reads:
            if b.w is not None:
                self._wait(q, b.w)
        for b in writes:
            if b.w is not None:
                self._wait(q, b.w)
            for t in b.rs.values():
                self._wait(q, t)
            for t in b.rd:
                self._wait(q, t)
        j = self.dcnt[q]
        slot = j % self.KD
        self._wait(q, self.dtoks[q][slot])
        ins = self.E[q].dma_start(out=out, in_=in_, **kw)
        ins.then_inc(self.dsem[q][slot], 16)
        tok = [f"d_{q}{slot}", self.dsem[q][slot], 16 * (j // self.KD + 1), "dma"]
        self.dtoks[q][slot] = tok
        self.dcnt[q] = j + 1
        self._record(q, tok, reads, writes, True)
        return tok

    def barrier(self):
        for e in self.CE:
            assert not self.pending[e], f"pending unsignaled instructions on {e} at barrier"
        toks = [self.last[e] for e in self.CE if e in self.last]
        for q in self.dq:
            toks += [t for t in self.dtoks[q] if t is not None]
        for e in self.E:
            for t in toks:
                self._wait(e, t)

    def finish(self):
        toks = []
        for q in self.dq:
            toks += [t for t in self.dtoks[q] if t is not None]
        for t in toks:
            self._wait("sp", t)


class Ring:
    def __init__(self, tiles):
        self.tiles = tiles
        self.bufs = [Buf() for _ in tiles]
        self.i = 0

    def next(self):
        k = self.i % len(self.tiles)
        self.i += 1
        return self.tiles[k], self.bufs[k]


def build_program(n_layers=DEPTH, debug=False, stop_after=None, n_sc=S // TS, n_grp=None):
    nc = bass.Bass("TRN2", target_bir_lowering=False)
    stack = contextlib.ExitStack()
    sc = Sched(nc, stack)

    def din(name, shape, dt=F32):
        return nc.dram_tensor(name, list(shape), dt, kind="ExternalInput").ap()

    dbg_names = set(debug) if debug else set()

    def dscr(name, shape, dt):
        kind = "ExternalOutput" if name in dbg_names else "Internal"
        return nc.dram_tensor(name, list(shape), dt, kind=kind).ap()

    x_in = din("x", [S, D])
    norm_g = din("norm_g", [DEPTH, D])
    w_in = din("w_in", [DEPTH, D, NIN])
    b_forget = din("b_forget", [DEPTH, 6])
    mla_q_g = din("mla_q_norm_g", [DEPTH, 512])
    w_uq = din("w_uq", [DEPTH, 512, 1152])
    mla_kv_g = din("mla_kv_norm_g", [DEPTH, 256])
    w_ukv = din("w_ukv", [DEPTH, 256, 1536])
    lq1 = din("lambda_q1", [DEPTH, 64])
    lk1 = din("lambda_k1", [DEPTH, 64])
    lq2 = din("lambda_q2", [DEPTH, 64])
    lk2 = din("lambda_k2", [DEPTH, 64])
    subln_g = din("diff_subln_g", [DEPTH, 128])
    rel_bias = din("rel_bias", [32, 4])
    w_out = din("w_out", [DEPTH, D, D])
    final_g = din("final_norm_g", [D])
    c_ident = din("c_ident", [128, 128], BF16)
    c_identf = din("c_identf", [128, 128], F32)
    c_ones = din("c_ones", [128, 128], BF16)
    c_cos = din("c_cos2", [64, S], F32)
    c_sin = din("c_sin2", [64, S], F32)
    c_maskA = din("c_maskA", [128, 128], F32)
    c_maskB = din("c_maskB", [128, 128], F32)
    c_bkt = din("c_bkt", [128, 32, 256], BF16)
    c_sel = din("c_sel", [6, 768], F32)
    c_role = din("c_role", [128, 2], F32)
    c_mA2 = din("c_mA2", [128, 256], F32)
    c_mB2 = din("c_mB2", [128, 256], F32)
    c_bkt3 = din("c_bkt3", [128, 32, 384], BF16)
    c_add3 = din("c_add3", [128, 384], F32)
    c_add2 = din("c_add2", [128, 256], F32)
    c_cos_o = din("c_cos_own", [64, S // 2], F32)
    c_sin_o = din("c_sin_own", [64, S // 2], F32)

    y_out = nc.dram_tensor("y", [S // 2, D], F32, kind="ExternalOutput").ap()

    X1 = dscr("X1", [S, D], F32)
    QA = dscr("QA", [6, 128, S], BF16)
    KA = dscr("KA", [6, 128, S], BF16)
    VA = dscr("VA", [S, 768], BF16)
    GA = dscr("GA", [6, 128, S], F32)
    NLF = dscr("NLF", [6, S], F32)
    QBN = dscr("QBN", [6, 128, S], BF16)
    QBR = dscr("QBR", [6, 64, S], BF16)
    KBN = dscr("KBN", [6, 128, S], BF16)
    KBR = dscr("KBR", [64, S], BF16)
    VB = dscr("VB", [S, 768], BF16)
    GB = dscr("GB", [6, 128, S], F32)
    QC = dscr("QC", [4, 128, S], BF16)
    KC = dscr("KC", [4, 128, S], BF16)
    VC = dscr("VC", [S, 512], BF16)
    GC = dscr("GC", [4, 128, S], F32)
    MIX = dscr("MIX", [16, 128, S], BF16)
    WPK = dscr("WPK", [40, 128, 16 * 256], BF16)
    dB = {n: Buf(n) for n in ("X1", "QA", "KA", "VA", "GA", "NLF", "QBN", "QBR", "KBN", "KBR", "VB", "GB",
                              "QC", "KC", "VC", "GC", "MIX", "Y")}
    bWPK = [Buf(f"WPK{i}") for i in range(40)]

    def sb(name, shape, dt):
        return stack.enter_context(nc.sbuf_tensor(name, list(shape), dt))

    ident = sb("ident", [128, 128], BF16)
    identf = sb("identf", [128, 128], F32)
    ones = sb("ones", [128, 128], BF16)
    maskA = sb("maskA", [128, 128], F32)
    maskB = sb("maskB", [128, 128], F32)
    sel = sb("sel", [6, 768], F32)
    rb_bc = sb("rb_bc", [128, 128], F32)
    role = sb("role", [128, 2], F32)
    bK = Buf("consts")
    for t, src in ((ident, c_ident), (identf, c_identf), (ones, c_ones), (maskA, c_maskA), (maskB, c_maskB)):
        sc.dma(t[:], src[:, :], writes=[bK])
    sc.dma(sel[:], c_sel[:, :], writes=[bK])
    sc.dma(role[:], c_role[:, :], writes=[bK])
    sc.dma(rb_bc[:], rel_bias.rearrange("a b -> (a b)").partition_broadcast(128), writes=[bK])

    if stop_after == ("S", 0):
        sc.finish()
        stack.close()
        return nc, sc

    evsel = [0]

    def ev_engine():
        evsel[0] ^= 1
        return "act" if evsel[0] else "dve"

    def copy_op(eng, out, in_):
        if eng == "act":
            return lambda e: e.copy(out=out, in_=in_)
        return lambda e: e.tensor_copy(out=out, in_=in_)

    for l in range(n_layers):
        xsrc = x_in if l == 0 else X1
        xsrc_b = Buf("xin") if l == 0 else dB["X1"]
        last = (l == DEPTH - 1)
        split = last
        NOWN = NT // 2 if split else NT
        ra = role[:, 0:1]
        rb_ = role[:, 1:2]
        lam_init = 0.8 - 0.6 * math.exp(-0.3 * l)

        with contextlib.ExitStack() as ps:
            def psb(name, shape, dt):
                return ps.enter_context(nc.sbuf_tensor(f"{name}_L{l}", list(shape), dt))

            def ppm(name, shape, dt=F32):
                return ps.enter_context(nc.psum_tensor(f"{name}_L{l}", list(shape), dt))

            g_bc = psb("g_bc", [128, D], F32)
            gq = psb("gq", [128, 4], F32)
            gkv = psb("gkv", [128, 2], F32)
            nbf = psb("nbf", [6, 1], F32)
            wuq_b = psb("wuq_b", [128, 4, 1536], BF16)
            wukv_b = psb("wukv_b", [128, 2, 1536], BF16)
            bW = Buf("pw")
            sc.dma(g_bc[:], norm_g[l:l + 1, :].partition_broadcast(128), writes=[bW])
            with nc.allow_non_contiguous_dma(reason="tiny param vectors"):
                sc.dma(gq[:], mla_q_g[l, :].rearrange("(k p) -> p k", p=128), writes=[bW])
                sc.dma(gkv[:], mla_kv_g[l, :].rearrange("(k p) -> p k", p=128), writes=[bW])
                sc.dma(nbf[:], b_forget[l, :].rearrange("(p o) -> p o", o=1), writes=[bW])
            sc.op("dve", lambda e: e.tensor_scalar(out=nbf[:], in0=nbf[:], scalar1=-1.0, scalar2=None, op0=ALU.mult),
                  reads=[bW], writes=[bW])
            with nc.sbuf_tensor(f"wuq_st_L{l}", [128, 4, 1152], F32) as wuq_st, \
                    nc.sbuf_tensor(f"wukv_st_L{l}", [128, 2, 1536], F32) as wukv_st:
                bst = Buf("wst0")
                sc.dma(wuq_st[:], w_uq[l].rearrange("(k p) n -> p k n", p=128), writes=[bst])
                sc.dma(wukv_st[:], w_ukv[l].rearrange("(k p) n -> p k n", p=128), writes=[bst])
                bW1, bW2 = Buf("pw1"), Buf("pw2")
                for h in range(6):
                    for (s0, s1, d0) in ((0, 192, 0), (160, 192, 192), (128, 160, 224)):
                        sc.op("act", lambda e, h=h, s0=s0, s1=s1, d0=d0: e.copy(
                            out=wuq_b[:, :, h * 256 + d0:h * 256 + d0 + (s1 - s0)],
                            in_=wuq_st[:, :, h * 192 + s0:h * 192 + s1]), reads=[bst], writes=[bW1])
                    sc.op("dve", lambda e, h=h: e.tensor_copy(
                        out=wukv_b[:, :, h * 128:h * 128 + 128], in_=wukv_st[:, :, h * 256:h * 256 + 128]),
                        reads=[bst], writes=[bW2])
                    sc.op("dve", lambda e, h=h: e.tensor_copy(
                        out=wukv_b[:, :, 768 + h * 128:768 + h * 128 + 128],
                        in_=wukv_st[:, :, h * 256 + 128:h * 256 + 256]), reads=[bst], writes=[bW2])
                sc.barrier()

            hT = psb("hT", [128, 16, TS], BF16)
            bhT = Buf("hT")
            wst = Ring([psb(f"wst{i}", [128, 16, 256], F32) for i in range(2)])
            wbr = Ring([psb(f"wb{i}", [128, 16, 256], BF16) for i in range(2)])
            wbr2 = [Buf("wb2_0"), Buf("wb2_1")]
            xst = Ring([psb(f"xst{i}", [128, D], F32) for i in range(2)])
            xs = psb("xs", [128, D], BF16)
            bxs = Buf("xs")
            ssq = Ring([psb(f"ssq{i}", [128, 1], F32) for i in range(2)])
            rstd = Ring([psb(f"rstd{i}", [128, 1], F32) for i in range(2)])
            st16 = Ring([psb(f"st16_{i}", [128, 512], BF16) for i in range(4)])
            st32 = Ring([psb(f"st32_{i}", [128, 512], F32) for i in range(3)])
            vst = Ring([psb(f"vst{i}", [128, 384], BF16) for i in range(3)])
            RAWN = 2048 if split else 4096
            raw = psb("raw", [128, RAWN], F32)
            braw = Buf("raw")
            sqr = Ring([psb(f"sq{i}", [128, 512], BF16) for i in range(2)])
            rbc = psb("rstd_bc", [128, TS], F32)
            brbc = Buf("rbc")
            rtmp = psb("rtmp", [128, 512], F32)
            brtmp = Buf("rtmp")
            cn = psb("cn", [128, RAWN], BF16)
            bcn = Buf("cn")
            cos_t = psb("cos_t", [64, TS], F32)
            sin_t = psb("sin_t", [64, TS], F32)
            bcs = Buf("cs")
            rp = Ring([psb(f"rp{i}", [64, 512], F32) for i in range(2)])
            if split:
                hTo = psb("hTo", [128, 16, 512], BF16)
                cos_o = psb("cos_o", [64, 512], F32)
                sin_o = psb("sin_o", [64, 512], F32)
            bhTo = Buf("hTo")

            def rawv(t, si, hs, nsub):
                st_ = RAWN // nsub
                return t[:, si * st_ + hs.start:si * st_ + hs.stop]
            aft = Ring([psb(f"aft{i}", [6, 512], F32) for i in range(2)])

            mp = Ring([ppm(f"mp{i}", [128, 512]) for i in range(4)])
            Rp = [ppm(f"Rp{i}", [128, 512]) for i in range(2)]
            bRp = [Buf("Rp0"), Buf("Rp1")]
            tp = Ring([ppm(f"tp{i}", [128, 512], BF16) for i in range(2)])

            groups = []
            for h0 in (0, 256, 512):
                groups.append((O_AQ + h0, 256, "fm16q", (QA, "QA", h0 // 128)))
            for h0 in (0, 256, 512):
                groups.append((O_AK + h0, 256, "fm16", (KA, "KA", h0 // 128)))
            for h0 in (0, 256, 512):
                groups.append((O_AV + h0, 256, "tm", (VA, "VA", h0)))
            groups.append((O_AF, 6, "af", None))
            for h0 in (0, 256, 512):
                groups.append((O_AG + h0, 256, "silu", (GA, "GA", h0 // 128)))
            groups.append((O_BCQ, 256, "cq", 0))
            groups.append((O_BCQ + 256, 256, "cq", 2))
            groups.append((O_BCKV, 256, "ckv", 0))
            groups.append((O_BKR, 128, "krope", None))
            for h0 in (0, 256, 512):
                groups.append((O_BG + h0, 256, "silu", (GB, "GB", h0 // 128)))
            for h0 in (0, 256):
                groups.append((O_CQ + h0, 256, "fm16q", (QC, "QC", h0 // 128)))
            for h0 in (0, 256):
                groups.append((O_CK + h0, 256, "fm16", (KC, "KC", h0 // 128)))
            for h0 in (0, 256):
                groups.append((O_CV + h0, 256, "tm", (VC, "VC", h0)))
            for h0 in (0, 256):
                groups.append((O_CG + h0, 256, "silu", (GC, "GC", h0 // 128)))

            w_l = w_in[l]

            def load_group(gi):
                c0, W, kind, _ = groups[gi]
                t, b = wst.next()
                if kind == "krope":
                    sc.dma(t[:, :, 0:64], w_l[:, c0:c0 + 64].rearrange("(k p) n -> p k n", p=128), writes=[b])
                    sc.dma(t[:, :, 64:96], w_l[:, c0 + 32:c0 + 64].rearrange("(k p) n -> p k n", p=128), writes=[b])
                    sc.dma(t[:, :, 96:128], w_l[:, c0:c0 + 32].rearrange("(k p) n -> p k n", p=128), writes=[b])
                else:
                    sc.dma(t[:, :, 0:W], w_l[:, c0:c0 + W].rearrange("(k p) n -> p k n", p=128), writes=[b])
                return t, b

            def fm_matmuls(pbank, pbuf, lhs_fn, nk, rhs_fn, m, ncols=512):
                for k in range(nk):
                    sc.op("pe", lambda e, k=k: e.matmul(pbank[0:m, 0:ncols], lhsT=lhs_fn(k), rhs=rhs_fn(k),
                                                         start=(k == 0), stop=(k == nk - 1)),
                          reads=[bhT, bhTo, bcn, bW], writes=[pbuf], sig=(k == nk - 1))

            def rope_combine(pa, ba, pb_, bb, cos_ap, sin_ap, dst_ap, dst_buf):
                t1, b1 = rp.next()
                t2, b2 = rp.next()
                sc.op("dve", lambda e: e.tensor_tensor(out=t1[:], in0=pa[0:64, :], in1=cos_ap, op=ALU.mult),
                      reads=[ba, bcs], writes=[b1])
                sc.op("dve", lambda e: e.tensor_tensor(out=t2[:], in0=pb_[0:64, :], in1=sin_ap, op=ALU.mult),
                      reads=[bb, bcs], writes=[b2])
                so, bo = st16.next()
                sc.op("pool", lambda e: e.tensor_tensor(out=so[0:64, :], in0=t1[:], in1=t2[:], op=ALU.add),
                      reads=[b1, b2], writes=[bo])
                sc.dma(dst_ap, so[0:64, :], reads=[bo], writes=[dst_buf])

            for sci in range(n_sc):
                tok0 = sci * TS
                ngroups = len(groups) if n_grp is None else n_grp
                packed = (sci > 0) and (n_grp is None)
                if packed:
                    def load_packed(gi):
                        t, b = wbr.next()
                        b2 = wbr2[(wbr.i - 1) % 2]
                        sc.dma(t[:], WPK[gi].rearrange("p (k n) -> p k n", k=16), reads=[bWPK[gi]], writes=[b, b2])
                        return t, b, b2
                    pre = [load_packed(0), load_packed(1)]
                else:
                    pre = [load_group(0), load_group(1)]
                sc.dma(cos_t[:], c_cos[:, tok0:tok0 + TS], writes=[bcs])
                sc.dma(sin_t[:], c_sin[:, tok0:tok0 + TS], writes=[bcs])
                xl = [None] * 8
                t, b = xst.next()
                sc.dma(t[:], xsrc[tok0:tok0 + 128, :], reads=[xsrc_b], writes=[b])
                xl[0] = (t, b)
                for tt in range(8):
                    if tt + 1 < 8:
                        t, b = xst.next()
                        r0 = tok0 + (tt + 1) * 128
                        sc.dma(t[:], xsrc[r0:r0 + 128, :], reads=[xsrc_b], writes=[b])
                        xl[tt + 1] = (t, b)
                    xt, bx = xl[tt]
                    sq_t, bsq = ssq.next()
                    rs_t, brs = rstd.next()
                    sc.op("act", lambda e, xt=xt, sq_t=sq_t: e.activation(out=xs[:], in_=xt[:], func=AF.Square,
                                                                           accum_out=sq_t[:]),
                          reads=[bx], writes=[bxs, bsq])
                    sc.op("act", lambda e, sq_t=sq_t: e.activation(out=sq_t[:], in_=sq_t[:], func=AF.Ln,
                                                                   scale=1.0 / D, bias=EPS),
                          reads=[bsq], writes=[bsq])
                    sc.op("act", lambda e, sq_t=sq_t, rs_t=rs_t: e.activation(out=rs_t[:], in_=sq_t[:], func=AF.Exp,
                                                                              scale=-0.5),
                          reads=[bsq], writes=[brs])
                    sc.op("dve", lambda e, xt=xt, rs_t=rs_t: e.scalar_tensor_tensor(
                        out=xs[:], in0=xt[:], scalar=rs_t[:, 0:1], in1=g_bc[:], op0=ALU.mult, op1=ALU.mult),
                        reads=[bx, brs, bW], writes=[bxs])
                    for q4 in range(4):
                        tpt, btp = tp.next()
                        for i in range(4):
                            dt_ = q4 * 4 + i
                            sc.op("pe", lambda e, dt_=dt_, i=i, tpt=tpt: e.transpose(
                                out=tpt[:, i * 128:(i + 1) * 128], in_=xs[:, dt_ * 128:(dt_ + 1) * 128],
                                identity=ident[:]),
                                reads=[bxs, bK], writes=[btp], sig=(i == 3))
                        eng = ev_engine()
                        sc.op(eng, copy_op(eng, hT[:, q4 * 4:q4 * 4 + 4, tt * 128:(tt + 1) * 128],
                                           tpt[:, 0:512].rearrange("p (a b) -> p a b", a=4)),
                              reads=[btp], writes=[bhT])

                if split:
                    sc.dma(cos_o[:], c_cos_o[:, sci * 512:sci * 512 + 512], writes=[bcs])
                    sc.dma(sin_o[:], c_sin_o[:, sci * 512:sci * 512 + 512], writes=[bcs])
                    for p4 in range(4):
                        sc.op("dve", lambda e, p4=p4: e.tensor_scalar(
                            out=hTo[:, :, p4 * 128:p4 * 128 + 128], in0=hT[:, :, 2 * p4 * 128:2 * p4 * 128 + 128],
                            scalar1=ra, scalar2=None, op0=ALU.mult), reads=[bhT, bK], writes=[bhTo])
                        sc.op("dve", lambda e, p4=p4: e.scalar_tensor_tensor(
                            out=hTo[:, :, p4 * 128:p4 * 128 + 128],
                            in0=hT[:, :, (2 * p4 + 1) * 128:(2 * p4 + 1) * 128 + 128], scalar=rb_,
                            in1=hTo[:, :, p4 * 128:p4 * 128 + 128], op0=ALU.mult, op1=ALU.add),
                            reads=[bhT, bhTo, bK], writes=[bhTo])

                def halves(side):
                    if split and side == "q":
                        return [(0, (lambda k: hTo[:, k, 0:512]), sci * 512, slice(0, 512), cos_o[:, :], sin_o[:, :])]
                    out_ = []
                    for half in range(2):
                        hs_ = slice(half * 512, half * 512 + 512)
                        out_.append((half, (lambda k, hs_=hs_: hT[:, k, hs_]), tok0 + half * 512, hs_,
                                     cos_t[:, hs_], sin_t[:, hs_]))
                    return out_

                for gi in range(ngroups):
                    c0, W, kind, info = groups[gi]
                    if packed:
                        wb_t, bwb, bwb2 = pre[gi % 2]
                    else:
                        wt, bwt = pre[gi % 2]
                        wb_t, bwb = wbr.next()
                        bwb2 = wbr2[(wbr.i - 1) % 2]
                        Wc = 128 if kind == "krope" else W
                        sc.op("act", lambda e, wt=wt, wb_t=wb_t, Wc=Wc: e.copy(out=wb_t[:, 0:8, 0:Wc],
                                                                                in_=wt[:, 0:8, 0:Wc]),
                              reads=[bwt], writes=[bwb])
                        sc.op("dve", lambda e, wt=wt, wb_t=wb_t, Wc=Wc: e.tensor_copy(out=wb_t[:, 8:16, 0:Wc],
                                                                                        in_=wt[:, 8:16, 0:Wc]),
                              reads=[bwt], writes=[bwb2])
                        if n_grp is None:
                            sc.dma(WPK[gi].rearrange("p (k n) -> p k n", k=16), wb_t[:], reads=[bwb, bwb2],
                                   writes=[bWPK[gi]])
                        if gi + 2 < ngroups:
                            pre[gi % 2] = load_group(gi + 2)

                    def main_mm(pb_, bpb, c_lo, m, rhs_fn, wb_t=wb_t, bwb=bwb, bwb2=bwb2):
                        for k in range(16):
                            sc.op("pe", lambda e, k=k: e.matmul(
                                pb_[0:m, :], lhsT=wb_t[:, k, c_lo:c_lo + m], rhs=rhs_fn(k),
                                start=(k == 0), stop=(k == 15)),
                                reads=[bhT, bhTo, bwb, bwb2], writes=[bpb], sig=(k == 15))

                    if kind in ("fm16", "fm16q", "silu"):
                        dst, dname, hb = info
                        side = "kv" if kind == "fm16" else "q"
                        for sub in range(W // 128):
                            for (hi, rhs_fn, t0, hs, _c, _s) in halves(side):
                                pb_, bpb = mp.next()
                                main_mm(pb_, bpb, sub * 128, 128, rhs_fn)
                                dsl = dst[hb + sub, :, t0:t0 + 512]
                                if kind != "silu":
                                    so, bo = st16.next()
                                    eng = ev_engine()
                                    sc.op(eng, copy_op(eng, so[:], pb_[:, :]), reads=[bpb], writes=[bo])
                                else:
                                    s1, bs1 = st32.next()
                                    sc.op("act", lambda e, s1=s1, pb_=pb_: e.activation(out=s1[:], in_=pb_[:, :],
                                                                                         func=AF.Exp, scale=-1.0),
                                          reads=[bpb], writes=[bs1])
                                    sc.op("act", lambda e, s1=s1: e.activation(out=s1[:], in_=s1[:], func=AF.Ln,
                                                                               bias=1.0),
                                          reads=[bs1], writes=[bs1])
                                    sc.op("act", lambda e, s1=s1: e.activation(out=s1[:], in_=s1[:], func=AF.Exp,
                                                                               scale=-1.0),
                                          reads=[bs1], writes=[bs1])
                                    so, bo = st32.next()
                                    sc.op("dve", lambda e, so=so, s1=s1, pb_=pb_: e.tensor_tensor(
                                        out=so[:], in0=pb_[:, :], in1=s1[:], op=ALU.mult),
                                        reads=[bpb, bs1], writes=[bo])
                                sc.dma(dsl, so[:], reads=[bo], writes=[dB[dname]])
                    elif kind == "tm":
                        dst, dname, cb = info
                        for tt in range(8):
                            pb_, bpb = mp.next()
                            for k in range(16):
                                sc.op("pe", lambda e, k=k, pb_=pb_, tt=tt, wb_t=wb_t: e.matmul(
                                    pb_[:, 0:256], lhsT=hT[:, k, tt * 128:tt * 128 + 128], rhs=wb_t[:, k, 0:256],
                                    start=(k == 0), stop=(k == 15)),
                                    reads=[bhT, bwb, bwb2], writes=[bpb], sig=(k == 15))
                            so, bo = vst.next()
                            eng = ev_engine()
                            sc.op(eng, copy_op(eng, so[:, 0:256], pb_[:, 0:256]), reads=[bpb], writes=[bo])
                            r0 = tok0 + tt * 128
                            sc.dma(dst[r0:r0 + 128, cb:cb + 256], so[:, 0:256], reads=[bo], writes=[dB[dname]])
                    elif kind == "af":
                        for (hi, rhs_fn, t0, hs, _c, _s) in halves("kv"):
                            pb_, bpb = mp.next()
                            main_mm(pb_, bpb, 0, 6, rhs_fn)
                            e1, be1 = aft.next()
                            e2, be2 = aft.next()
                            sc.op("act", lambda e, e1=e1, pb_=pb_: e.activation(
                                out=e1[:], in_=pb_[0:6, :], func=AF.Exp, scale=-1.0, bias=nbf[:, 0:1]),
                                reads=[bpb, bW], writes=[be1])
                            sc.op("act", lambda e, e1=e1, e2=e2: e.activation(
                                out=e2[:], in_=e1[:], func=AF.Ln, bias=1.0), reads=[be1], writes=[be2])
                            sc.dma(NLF[:, t0:t0 + 512], e2[:], reads=[be2], writes=[dB["NLF"]])
                    elif kind in ("cq", "ckv"):
                        sub0 = info
                        nr = 4 if kind == "cq" else 2
                        gvec = gq if kind == "cq" else gkv
                        side = "q" if kind == "cq" else "kv"
                        hv = halves(side)
                        defR = []
                        for sub in range(2):
                            si = sub0 + sub
                            for (hi, rhs_fn, t0, hs, _c, _s) in hv:
                                pb_, bpb = mp.next()
                                main_mm(pb_, bpb, sub * 128, 128, rhs_fn)
                                for fn_ in defR:
                                    fn_()
                                defR = []
                                sc.op("dve", lambda e, pb_=pb_, si=si, hs=hs: e.tensor_copy(
                                    out=rawv(raw, si, hs, nr), in_=pb_[:, :]), reads=[bpb], writes=[braw])
                                sq_t, bsq = sqr.next()
                                sc.op("act", lambda e, si=si, hs=hs, sq_t=sq_t: e.activation(
                                    out=sq_t[:], in_=rawv(raw, si, hs, nr), func=AF.Square),
                                    reads=[braw], writes=[bsq])
                                defR.append(lambda sq_t=sq_t, bsq=bsq, hi=hi, si=si: sc.op(
                                    "pe", lambda e: e.matmul(
                                        Rp[hi][:, :], lhsT=ones[:], rhs=sq_t[:], start=(si == 0), stop=(si == nr - 1)),
                                    reads=[bsq, bK], writes=[bRp[hi]], sig=True))
                        for fn_ in defR:
                            fn_()
                        if sub0 + 2 == nr:
                            for (hi, rhs_fn, t0, hs, _c, _s) in hv:
                                sc.op("act", lambda e, hi=hi: e.activation(
                                    out=rtmp[:], in_=Rp[hi][:, :], func=AF.Ln, scale=1.0 / (128 * nr), bias=EPS),
                                    reads=[bRp[hi]], writes=[brtmp])
                                sc.op("act", lambda e, hs=hs: e.activation(out=rbc[:, hs], in_=rtmp[:], func=AF.Exp,
                                                                           scale=-0.5),
                                      reads=[brtmp], writes=[brbc])
                                for si in range(nr):
                                    sc.op("dve", lambda e, si=si, hs=hs: e.scalar_tensor_tensor(
                                        out=rawv(cn, si, hs, nr), in0=rawv(raw, si, hs, nr), scalar=gvec[:, si:si + 1],
                                        in1=rbc[:, hs], op0=ALU.mult, op1=ALU.mult),
                                        reads=[braw, brbc, bW], writes=[bcn])
                            if kind == "cq":
                                for h in range(6):
                                    for (hi, rhs_fn, t0, hs, cos_ap, sin_ap) in hv:
                                        ts_ = slice(t0, t0 + 512)
                                        pb_, bpb = mp.next()
                                        fm_matmuls(pb_, bpb, lambda k, h=h: wuq_b[:, k, h * 256:h * 256 + 128], 4,
                                                   lambda k, hs=hs: rawv(cn, k, hs, 4), 128)
                                        so, bo = st16.next()
                                        eng = ev_engine()
                                        sc.op(eng, copy_op(eng, so[:], pb_[:, :]), reads=[bpb], writes=[bo])
                                        sc.dma(QBN[h, :, ts_], so[:], reads=[bo], writes=[dB["QBN"]])
                                        pa, ba = mp.next()
                                        fm_matmuls(pa, ba, lambda k, h=h: wuq_b[:, k, h * 256 + 128:h * 256 + 192], 4,
                                                   lambda k, hs=hs: rawv(cn, k, hs, 4), 64)
                                        pc, bc = mp.next()
                                        fm_matmuls(pc, bc, lambda k, h=h: wuq_b[:, k, h * 256 + 192:h * 256 + 256], 4,
                                                   lambda k, hs=hs: rawv(cn, k, hs, 4), 64)
                                        rope_combine(pa, ba, pc, bc, cos_ap, sin_ap, QBR[h, :, ts_], dB["QBR"])
                            else:
                                for h in range(6):
                                    for (hi, rhs_fn, t0, hs, _c, _s) in hv:
                                        ts_ = slice(t0, t0 + 512)
                                        pb_, bpb = mp.next()
                                        fm_matmuls(pb_, bpb, lambda k, h=h: wukv_b[:, k, h * 128:h * 128 + 128], 2,
                                                   lambda k, hs=hs: rawv(cn, k, hs, 2), 128)
                                        so, bo = st16.next()
                                        eng = ev_engine()
                                        sc.op(eng, copy_op(eng, so[:], pb_[:, :]), reads=[bpb], writes=[bo])
                                        sc.dma(KBN[h, :, ts_], so[:], reads=[bo], writes=[dB["KBN"]])
                                for tt in range(8):
                                    tsl = slice(tt * 128, tt * 128 + 128)
                                    for cg in range(2):
                                        pb_, bpb = mp.next()
                                        for k in range(2):
                                            sc.op("pe", lambda e, k=k, pb_=pb_, tsl=tsl, cg=cg: e.matmul(
                                                pb_[:, 0:384], lhsT=rawv(cn, k, tsl, 2),
                                                rhs=wukv_b[:, k, 768 + cg * 384:768 + cg * 384 + 384],
                                                start=(k == 0), stop=(k == 1)),
                                                reads=[bcn, bW], writes=[bpb], sig=(k == 1))
                                        so, bo = vst.next()
                                        eng = ev_engine()
                                        sc.op(eng, copy_op(eng, so[:, 0:384], pb_[:, 0:384]), reads=[bpb], writes=[bo])
                                        r0 = tok0 + tt * 128
                                        sc.dma(VB[r0:r0 + 128, cg * 384:cg * 384 + 384], so[:, 0:384], reads=[bo],
                                               writes=[dB["VB"]])
                    elif kind == "krope":
                        for (hi, rhs_fn, t0, hs, cos_ap, sin_ap) in halves("kv"):
                            ts_ = slice(t0, t0 + 512)
                            pa, ba = mp.next()
                            main_mm(pa, ba, 0, 64, rhs_fn)
                            pc, bc = mp.next()
                            main_mm(pc, bc, 64, 64, rhs_fn)
                            rope_combine(pa, ba, pc, bc, cos_ap, sin_ap, KBR[:, ts_], dB["KBR"])
                    if packed and gi + 2 < ngroups:
                        pre[gi % 2] = load_packed(gi + 2)
            sc.barrier()
        if stop_after == ("P", l):
            break

        with contextlib.ExitStack() as ps:
            def psb(name, shape, dt):
                return ps.enter_context(nc.sbuf_tensor(f"{name}_L{l}", list(shape), dt))

            def ppm(name, shape, dt=F32):
                return ps.enter_context(nc.psum_tensor(f"{name}_L{l}", list(shape), dt))

            QT = Ring([psb(f"QT{i}", [128, S], BF16) for i in range(2)])
            KT = Ring([psb(f"KT{i}", [128, S], BF16) for i in range(2)])
            VT = Ring([psb(f"VT{i}", [128, NT, 128], BF16) for i in range(2)])
            QR = Ring([psb(f"QR{i}", [64, S], BF16) for i in range(2)])
            KR = psb("KR", [64, S], BF16)
            bKR = Buf("KR")
            gch = Ring([psb(f"gch{i}", [128, 512], F32) for i in range(2)])
            PT = Ring([psb(f"PT{i}", [128, 512], BF16) for i in range(4)])
            tmpS = Ring([psb(f"tmpS{i}", [128, 128], F32) for i in range(6)])
            ep = Ring([psb(f"ep{i}", [128, 512], F32) for i in range(6)])
            mx16 = Ring([psb(f"mx{i}", [128, 512], BF16) for i in range(2)])
            sq16 = psb("sq16", [128, 512], BF16)
            bsq16 = Buf("sq16")
            nlf = psb("nlf", [6, S], F32)
            ncum = psb("ncum", [6, S], F32)
            ones6 = psb("ones6", [6, 1024], F32)
            ncumT = psb("ncumT", [128, NT, 6], F32)
            ncref = psb("ncref", [128, 6, NT], F32)
            biasA = Ring([psb(f"biasA{i}", [128, NT, NT], F32) for i in range(2)])
            biasC = Ring([psb(f"biasC{i}", [128, NT, 8], F32) for i in range(2)])
            fAr = Ring([psb(f"fA{i}", [128, NT], F32) for i in range(2)])
            lamt = psb("lamt", [128, 4, 64], F32)
            lamj = psb("lamj", [128, 64], F32)
            lamv = psb("lamv", [128, 4], F32)
            gsub = psb("gsub", [128, 1], F32)
            bL = Buf("lam")
            bcum = Buf("cum")
            NB3 = 384 if split else 256
            BT = psb("BT", [128, 4, NB3], F32)
            bBT = Buf("BT")
            mA2 = psb("mA2", [128, 256], F32)
            mB2 = psb("mB2", [128, 256], F32)
            if split:
                sc.dma(mA2[:], c_mA2[:, :], writes=[bBT])
                sc.dma(mB2[:], c_mB2[:, :], writes=[bBT])
            with nc.sbuf_tensor(f"bkt_L{l}", [128, 32, NB3], BF16) as bkt, \
                    nc.sbuf_tensor(f"addm_L{l}", [128, NB3], F32) as addm:
                bbk = Buf("bkt")
                sc.dma(bkt[:], (c_bkt3 if split else c_bkt)[:, :, :], writes=[bbk])
                sc.dma(addm[:], (c_add3 if split else c_add2)[:, :], writes=[bbk])
                for h in range(4):
                    for b in range(32):
                        col = rb_bc[:, b * 4 + h:b * 4 + h + 1]
                        if b == 0:
                            sc.op("dve", lambda e, h=h, b=b, col=col: e.tensor_scalar(
                                out=BT[:, h, :], in0=bkt[:, b, :], scalar1=col, scalar2=None, op0=ALU.mult),
                                reads=[bbk, bK], writes=[bBT])
                        else:
                            sc.op("dve", lambda e, h=h, b=b, col=col: e.scalar_tensor_tensor(
                                out=BT[:, h, :], in0=bkt[:, b, :], scalar=col, in1=BT[:, h, :], op0=ALU.mult,
                                op1=ALU.add), reads=[bbk, bK, bBT], writes=[bBT])
                    sc.op("dve", lambda e, h=h: e.tensor_tensor(
                        out=BT[:, h, :], in0=BT[:, h, :], in1=addm[:], op=ALU.add),
                        reads=[bBT, bbk], writes=[bBT])
                sc.barrier()

            Sb = Ring([ppm(f"Sb{i}", [128, 512]) for i in range(4)])
            OL = Ring([ppm(f"OL{i}", [128, 512]) for i in range(4)])

            for i, src in enumerate((lq1, lk1, lq2, lk2)):
                sc.dma(lamt[:, i, :], src[l:l + 1, :].partition_broadcast(128), writes=[bL])
            with nc.allow_non_contiguous_dma(reason="tiny param vectors"):
                sc.dma(gsub[:], subln_g[l, :].rearrange("(p o) -> p o", o=1), writes=[bL])
            sc.op("dve", lambda e: e.scalar_tensor_tensor(out=lamj[:], in0=lamt[:, 0, :], scalar=1.0, in1=lamt[:, 1, :],
                                                          op0=ALU.mult, op1=ALU.mult, accum_out=lamv[:, 0:1]),
                  reads=[bL], writes=[bL])
            sc.op("dve", lambda e: e.scalar_tensor_tensor(out=lamj[:], in0=lamt[:, 2, :], scalar=1.0, in1=lamt[:, 3, :],
                                                          op0=ALU.mult, op1=ALU.mult, accum_out=lamv[:, 1:2]),
                  reads=[bL], writes=[bL])
            sc.op("act", lambda e: e.activation(out=lamv[:, 0:2], in_=lamv[:, 0:2], func=AF.Exp), reads=[bL],
                  writes=[bL])
            sc.op("dve", lambda e: e.tensor_tensor(out=lamv[:, 2:3], in0=lamv[:, 1:2], in1=lamv[:, 0:1],
                                                   op=ALU.subtract), reads=[bL], writes=[bL])
            sc.op("dve", lambda e: e.tensor_scalar(out=lamv[:, 2:3], in0=lamv[:, 2:3], scalar1=-lam_init, scalar2=None,
                                                   op0=ALU.add), reads=[bL], writes=[bL])
            sc.op("dve", lambda e: e.tensor_scalar(out=lamv[:, 3:4], in0=gsub[:, 0:1], scalar1=1.0 - lam_init,
                                                   scalar2=None, op0=ALU.mult), reads=[bL], writes=[bL])
            neglam = lamv[:, 2:3]
            gsub_s = lamv[:, 3:4]

            sc.dma(nlf[:], NLF[:, :], reads=[dB["NLF"]], writes=[bcum])
            sc.op("pool", lambda e: e.memset(ones6[:], 1.0), writes=[bcum])
            for c in range(4):
                init = 0.0 if c == 0 else ncum[:, c * 1024 - 1:c * 1024]
                sc.op("dve", lambda e, c=c, init=init: e.tensor_tensor_scan(
                    out=ncum[:, c * 1024:(c + 1) * 1024], data0=ones6[:], data1=nlf[:, c * 1024:(c + 1) * 1024],
                    initial=init, op0=ALU.mult, op1=ALU.add), reads=[bcum], writes=[bcum])
            for kt in range(NT):
                pb_, bpb = Sb.next()
                sc.op("pe", lambda e, kt=kt, pb_=pb_: e.transpose(out=pb_[:, 0:6], in_=ncum[:, kt * 128:(kt + 1) * 128],
                                                                  identity=identf[0:6, 0:6]),
                      reads=[bcum, bK], writes=[bpb])
                sc.op("dve", lambda e, kt=kt, pb_=pb_: e.tensor_copy(out=ncumT[:, kt, :], in_=pb_[:, 0:6]),
                      reads=[bpb], writes=[bcum])
            for h in range(6):
                pb_, bpb = Sb.next()
                sc.op("pe", lambda e, h=h, pb_=pb_: e.matmul(
                    pb_[:, 0:NT], lhsT=sel[:, h * 128:(h + 1) * 128],
                    rhs=ncum[:, :].rearrange("p (a b) -> p a b", b=128)[:, :, 0], start=True, stop=True),
                    reads=[bcum, bK], writes=[bpb])
                sc.op("dve", lambda e, h=h, pb_=pb_: e.tensor_copy(out=ncref[:, h, :], in_=pb_[:, 0:NT]),
                      reads=[bpb], writes=[bcum])

            t_stop = stop_after[0] if (stop_after and stop_after[1] == l) else None
            nA = 0 if t_stop == "T0" else (int(os.environ.get("KNA", 6)))
            nB = 0 if t_stop in ("T0", "TA") else int(os.environ.get("KNB", 6))
            nC = 0 if t_stop in ("T0", "TA", "TB") else int(os.environ.get("KNC", 4))
            nJ = int(os.environ.get("KNJ", NOWN // 4))
            NQ = NOWN * 128

            def nkt_of(j):
                return (8 * j + 8) if split else (4 * j + 4)

            def c0_of(kt, j):
                i_min = (kt // 2) if split else kt
                return max(0, i_min - 4 * j) * 128

            def special(kt, i, with_prev):
                d = kt - (2 * i if split else i)
                hi = 1 if split else 0
                lo = -1 if with_prev else 0
                return (d + 1) if lo <= d <= hi else None

            if split:
                ncro = psb("ncro", [128, 6, NOWN], F32)
                nv = ncref[:].rearrange("p h (i two) -> p h i two", two=2)
                sc.op("dve", lambda e: e.tensor_scalar(out=ncro[:], in0=nv[:, :, :, 0], scalar1=ra, scalar2=None,
                                                       op0=ALU.mult), reads=[bcum, bK], writes=[bcum])
                sc.op("dve", lambda e: e.scalar_tensor_tensor(out=ncro[:], in0=nv[:, :, :, 1], scalar=rb_, in1=ncro[:],
                                                              op0=ALU.mult, op1=ALU.add),
                      reads=[bcum, bK], writes=[bcum])
            else:
                ncro = ncref
            mskA = (lambda idx: mA2[:, (idx - 1) * 128:(idx - 1) * 128 + 128]) if split else (lambda idx: maskA[:])
            mskB = (lambda idx: mB2[:, (idx - 1) * 128:(idx - 1) * 128 + 128]) if split else (lambda idx: maskB[:])

            def epilogue_AB(Ob, bO, Lb, bLb, gsrc, gname, h, j, ft):
                cs = slice(j * 512, j * 512 + 512)
                gt, bg = gch.next()
                sc.dma(gt[:], gsrc[h, :, cs], reads=[dB[gname]], writes=[bg])
                r1, b1 = ep.next()
                sc.op("dve", lambda e: e.reciprocal(out=r1[:], in_=Lb[:, 0:512]), reads=[bLb], writes=[b1])
                r2, b2 = ep.next()
                sc.op("dve", lambda e: e.tensor_tensor(out=r2[:], in0=Ob[:, 0:512], in1=r1[:], op=ALU.mult),
                      reads=[bO, b1], writes=[b2])
                mo, bm = mx16.next()
                sc.op("pool", lambda e: e.tensor_tensor(out=mo[:], in0=r2[:], in1=gt[:], op=ALU.mult),
                      reads=[b2, bg], writes=[bm])
                sc.dma(MIX[ft, :, cs], mo[:], reads=[bm], writes=[dB["MIX"]])

            scaleA = 1.0 / math.sqrt(128.0)

            def load_head_A(h):
                q, bq = QT.next()
                k, bk = KT.next()
                v, bv = VT.next()
                sc.dma(q[:, 0:NQ], QA[h, :, 0:NQ], reads=[dB["QA"]], writes=[bq])
                sc.dma(k[:], KA[h, :, :], reads=[dB["KA"]], writes=[bk])
                sc.dma(v[:], VA[:, h * 128:(h + 1) * 128].rearrange("(t p) d -> p t d", p=128), reads=[dB["VA"]],
                       writes=[bv])
                return (q, bq, k, bk, v, bv)

            def prep_A(h):
                bA, bbA = biasA.next()
                bC, bbC = biasC.next()
                fA, bfA = fAr.next()
                NJ_ = NOWN // 4
                for kt in range(NT):
                    sc.op("dve", lambda e, kt=kt, h=h, bA=bA: e.tensor_scalar(
                        out=bA[:, kt, 0:NOWN], in0=ncro[:, h, :], scalar1=-1.0, scalar2=ncumT[:, kt, h:h + 1],
                        op0=ALU.mult, op1=ALU.add), reads=[bcum], writes=[bbA])
                    sc.op("dve", lambda e, kt=kt, h=h, bC=bC: e.tensor_scalar(
                        out=bC[:, kt, 0:NJ_], in0=ncro[:, h, :].rearrange("p (j f) -> p j f", f=4)[:, :, 0],
                        scalar1=-1.0, scalar2=ncumT[:, kt, h:h + 1], op0=ALU.mult, op1=ALU.add),
                        reads=[bcum], writes=[bbC])
                for j in range(NJ_):
                    sc.op("dve", lambda e, j=j, h=h, fA=fA: e.tensor_scalar(
                        out=fA[:, 4 * j:4 * j + 4], in0=ncro[:, h, 4 * j:4 * j + 4], scalar1=-1.0,
                        scalar2=ncro[:, h, 4 * j:4 * j + 1], op0=ALU.mult, op1=ALU.add), reads=[bcum], writes=[bfA])
                return (bA, bbA, bC, bbC, fA, bfA)

            prepA = prep_A(0) if nA > 0 else None
            nxt = load_head_A(0)
            for h in range(nA):
                q, bq, k, bk, v, bv = nxt
                if h + 1 < nA:
                    nxt = load_head_A(h + 1)
                bA, bbA, bC, bbC, fA, bfA = prepA
                sc.op("act", lambda e, fA=fA: e.activation(out=fA[:, 0:NOWN], in_=fA[:, 0:NOWN], func=AF.Exp),
                      reads=[bfA], writes=[bfA])
                for j in range(nJ):
                    if j == max(nJ - 1, 0) and h + 1 < nA:
                        prepA_next = prep_A(h + 1)
                    Oo, bOo = OL.next()
                    Lo, bLo = OL.next()
                    Od, bOd = OL.next()
                    Ld, bLd = OL.next()
                    nkt = nkt_of(j)
                    n_off = (8 * j) if split else (4 * j)
                    pend = None
                    for kt in range(nkt + 1):
                        if kt < nkt:
                            off = kt < n_off
                            c0 = 0 if off else c0_of(kt, j)
                            Sp, bS = Sb.next()
                            sc.op("pe", lambda e, kt=kt, c0=c0, Sp=Sp: e.matmul(
                                Sp[:, c0:512], lhsT=k[:, kt * 128:(kt + 1) * 128],
                                rhs=q[:, j * 512 + c0:j * 512 + 512], start=True, stop=True),
                                reads=[bk, bq], writes=[bS])
                            pt, bpt = PT.next()
                            if off:
                                sc.op("act", lambda e, Sp=Sp, pt=pt, kt=kt, j=j: e.activation(
                                    out=pt[:, 0:512], in_=Sp[:, 0:512], func=AF.Exp, bias=bC[:, kt, j:j + 1],
                                    scale=scaleA), reads=[bS, bbC], writes=[bpt])
                            else:
                                for t in range(c0 // 128, 4):
                                    qb = 4 * j + t
                                    cs = slice(t * 128, t * 128 + 128)
                                    bias = bA[:, kt, qb:qb + 1]
                                    sidx = special(kt, qb, False)
                                    if sidx is not None:
                                        tm, btm = tmpS.next()
                                        sc.op("dve", lambda e, tm=tm, Sp=Sp, cs=cs, sidx=sidx: e.scalar_tensor_tensor(
                                            out=tm[:], in0=Sp[:, cs], scalar=scaleA, in1=mskA(sidx), op0=ALU.mult,
                                            op1=ALU.add), reads=[bK, bBT], writes=[btm, bS])
                                        sc.op("act", lambda e, tm=tm, pt=pt, cs=cs, bias=bias: e.activation(
                                            out=pt[:, cs], in_=tm[:], func=AF.Exp, bias=bias, scale=1.0),
                                            reads=[btm, bbA], writes=[bpt])
                                    else:
                                        sc.op("act", lambda e, Sp=Sp, pt=pt, cs=cs, bias=bias: e.activation(
                                            out=pt[:, cs], in_=Sp[:, cs], func=AF.Exp, bias=bias, scale=scaleA),
                                            reads=[bS, bbA], writes=[bpt])
                            cur = (kt, c0, pt, bpt, off)
                        else:
                            cur = None
                        if pend is not None:
                            pk, pc0, ppt, pbpt, poff = pend
                            Ot, bOt, Lt, bLt = (Oo, bOo, Lo, bLo) if poff else (Od, bOd, Ld, bLd)
                            st_ = (pk == 0) if poff else (pk == n_off)
                            sp_ = (pk == n_off - 1) if poff else (pk == nkt - 1)
                            sc.op("pe", lambda e, pk=pk, pc0=pc0, ppt=ppt, Ot=Ot, st_=st_, sp_=sp_: e.matmul(
                                Ot[:, pc0:512], lhsT=v[:, pk, :], rhs=ppt[:, pc0:512], start=st_, stop=sp_),
                                reads=[bv, pbpt], writes=[bOt], sig=False)
                            sc.op("pe", lambda e, pk=pk, pc0=pc0, ppt=ppt, Lt=Lt, st_=st_, sp_=sp_: e.matmul(
                                Lt[:, pc0:512], lhsT=ones[:], rhs=ppt[:, pc0:512], start=st_, stop=sp_),
                                reads=[bK, pbpt], writes=[bLt])
                        pend = cur
                    if n_off == 0:
                        epilogue_AB(Od, bOd, Ld, bLd, GA, "GA", h, j, h)
                    else:
                        to, bto = ep.next()
                        tl, btl = ep.next()
                        for t in range(4):
                            i = 4 * j + t
                            cs = slice(t * 128, t * 128 + 128)
                            sc.op("dve", lambda e, cs=cs, i=i, to=to: e.tensor_scalar(
                                out=to[:, cs], in0=Oo[:, cs], scalar1=fA[:, i:i + 1], scalar2=None, op0=ALU.mult),
                                reads=[bOo, bfA], writes=[bto])
                            sc.op("dve", lambda e, cs=cs, i=i, tl=tl: e.tensor_scalar(
                                out=tl[:, cs], in0=Lo[:, cs], scalar1=fA[:, i:i + 1], scalar2=None, op0=ALU.mult),
                                reads=[bLo, bfA], writes=[btl])
                        sc.op("dve", lambda e, to=to: e.tensor_tensor(out=to[:], in0=Od[:, :], in1=to[:], op=ALU.add),
                              reads=[bOd, bto], writes=[bto])
                        sc.op("dve", lambda e, tl=tl: e.tensor_tensor(out=tl[:], in0=Ld[:, :], in1=tl[:], op=ALU.add),
                              reads=[bLd, btl], writes=[btl])
                        epilogue_AB(to, bto, tl, btl, GA, "GA", h, j, h)
                if h + 1 < nA:
                    prepA = prepA_next

            scaleB = 1.0 / math.sqrt(192.0)
            sc.dma(KR[:], KBR[:, :], reads=[dB["KBR"]], writes=[bKR])

            def load_head_B(h):
                q, bq = QT.next()
                k, bk = KT.next()
                v, bv = VT.next()
                qr, bqr = QR.next()
                sc.dma(q[:, 0:NQ], QBN[h, :, 0:NQ], reads=[dB["QBN"]], writes=[bq])
                sc.dma(k[:], KBN[h, :, :], reads=[dB["KBN"]], writes=[bk])
                sc.dma(qr[:, 0:NQ], QBR[h, :, 0:NQ], reads=[dB["QBR"]], writes=[bqr])
                sc.dma(v[:], VB[:, h * 128:(h + 1) * 128].rearrange("(t p) d -> p t d", p=128), reads=[dB["VB"]],
                       writes=[bv])
                return (q, bq, k, bk, v, bv, qr, bqr)

            nxt = load_head_B(0)
            for h in range(nB):
                q, bq, k, bk, v, bv, qr, bqr = nxt
                if h + 1 < nB:
                    nxt = load_head_B(h + 1)
                for j in range(nJ):
                    Ob, bO = OL.next()
                    Lb, bLb = OL.next()
                    nkt = nkt_of(j)
                    pend = None
                    for kt in range(nkt + 1):
                        if kt < nkt:
                            c0 = c0_of(kt, j)
                            Sp, bS = Sb.next()
                            sc.op("pe", lambda e, kt=kt, c0=c0, Sp=Sp: e.matmul(
                                Sp[:, c0:512], lhsT=k[:, kt * 128:(kt + 1) * 128],
                                rhs=q[:, j * 512 + c0:j * 512 + 512], start=True, stop=False),
                                reads=[bk, bq], writes=[bS], sig=False)
                            sc.op("pe", lambda e, kt=kt, c0=c0, Sp=Sp: e.matmul(
                                Sp[:, c0:512], lhsT=KR[:, kt * 128:(kt + 1) * 128],
                                rhs=qr[:, j * 512 + c0:j * 512 + 512], start=False, stop=True),
                                reads=[bKR, bqr], writes=[bS])
                            pt, bpt = PT.next()
                            c1 = c0
                            sidx = special(kt, 4 * j + c0 // 128, False)
                            if sidx is not None:
                                cs = slice(c0, c0 + 128)
                                tm, btm = tmpS.next()
                                sc.op("dve", lambda e, tm=tm, Sp=Sp, cs=cs, sidx=sidx: e.scalar_tensor_tensor(
                                    out=tm[:], in0=Sp[:, cs], scalar=scaleB, in1=mskB(sidx), op0=ALU.mult,
                                    op1=ALU.add), reads=[bK, bBT], writes=[btm, bS])
                                sc.op("act", lambda e, tm=tm, pt=pt, cs=cs: e.activation(
                                    out=pt[:, cs], in_=tm[:], func=AF.Exp), reads=[btm], writes=[bpt])
                                c1 = c0 + 128
                            if c1 < 512:
                                sc.op("act", lambda e, Sp=Sp, pt=pt, c1=c1: e.activation(
                                    out=pt[:, c1:512], in_=Sp[:, c1:512], func=AF.Exp, scale=scaleB),
                                    reads=[bS], writes=[bpt])
                            cur = (kt, c0, pt, bpt)
                        else:
                            cur = None
                        if pend is not None:
                            pk, pc0, ppt, pbpt = pend
                            sc.op("pe", lambda e, pk=pk, pc0=pc0, ppt=ppt: e.matmul(
                                Ob[:, pc0:512], lhsT=v[:, pk, :], rhs=ppt[:, pc0:512], start=(pk == 0),
                                stop=(pk == nkt - 1)), reads=[bv, pbpt], writes=[bO], sig=False)
                            sc.op("pe", lambda e, pk=pk, pc0=pc0, ppt=ppt: e.matmul(
                                Lb[:, pc0:512], lhsT=ones[:], rhs=ppt[:, pc0:512], start=(pk == 0),
                                stop=(pk == nkt - 1)), reads=[bK, pbpt], writes=[bLb])
                        pend = cur
                    epilogue_AB(Ob, bO, Lb, bLb, GB, "GB", h, j, 6 + h)

            scaleC = 1.0 / 8.0

            def load_head_C(h):
                q, bq = QT.next()
                k, bk = KT.next()
                v, bv = VT.next()
                sc.dma(q[:, 0:NQ], QC[h, :, 0:NQ], reads=[dB["QC"]], writes=[bq])
                sc.dma(k[:], KC[h, :, :], reads=[dB["KC"]], writes=[bk])
                sc.dma(v[:], VC[:, h * 128:(h + 1) * 128].rearrange("(t p) d -> p t d", p=128), reads=[dB["VC"]],
                       writes=[bv])
                return (q, bq, k, bk, v, bv)

            pendC = []

            def flushC(upto=3):
                if not pendC:
                    return
                st = pendC[0]
                while st and (4 - len(st)) <= upto:
                    stage = 4 - len(st)
                    fn_ = st.pop(0)
                    if stage == 2:
                        cstate[0] = fn_()
                    elif stage == 3:
                        fn_(cstate[0])
                    else:
                        fn_()
                    if 4 - len(st) > upto:
                        break
                if not st:
                    pendC.pop(0)

            cstate = [None]

            nxt = load_head_C(0)
            for h in range(nC):
                q, bq, k, bk, v, bv = nxt
                if h + 1 < nC:
                    nxt = load_head_C(h + 1)
                b15 = rb_bc[:, 60 + h:61 + h]
                for j in range(nJ):
                    Os = [OL.next() for _ in range(4)]
                    nkt = nkt_of(j)
                    pend = None
                    for kt in range(nkt + 1):
                        if kt < nkt:
                            c0 = c0_of(kt, j)
                            for (kk, stg) in ((6, 1), (8, 3)):
                                if kt == min(kk, nkt - 1):
                                    flushC(stg)
                            pts = []
                            for m in range(2):
                                Sp, bS = Sb.next()
                                ps_ = slice(m * 64, m * 64 + 64)
                                sc.op("pe", lambda e, kt=kt, c0=c0, Sp=Sp, ps_=ps_: e.matmul(
                                    Sp[:, c0:512], lhsT=k[ps_, kt * 128:(kt + 1) * 128],
                                    rhs=q[ps_, j * 512 + c0:j * 512 + 512], start=True, stop=True),
                                    reads=[bk, bq], writes=[bS])
                                pt, bpt = PT.next()
                                cfar = 512
                                for t in range(c0 // 128, 4):
                                    qb = 4 * j + t
                                    cs = slice(t * 128, t * 128 + 128)
                                    off = special(kt, qb, True)
                                    if off is not None:
                                        tm, btm = tmpS.next()
                                        sc.op("dve", lambda e, tm=tm, Sp=Sp, cs=cs, off=off: e.scalar_tensor_tensor(
                                            out=tm[:], in0=Sp[:, cs], scalar=scaleC,
                                            in1=BT[:, h, off * 128:off * 128 + 128], op0=ALU.mult, op1=ALU.add),
                                            reads=[bBT], writes=[btm, bS])
                                        sc.op("act", lambda e, tm=tm, pt=pt, cs=cs: e.activation(
                                            out=pt[:, cs], in_=tm[:], func=AF.Exp), reads=[btm], writes=[bpt])
                                    else:
                                        cfar = min(cfar, t * 128)
                                if cfar < 512:
                                    sc.op("act", lambda e, Sp=Sp, pt=pt, cfar=cfar: e.activation(
                                        out=pt[:, cfar:512], in_=Sp[:, cfar:512], func=AF.Exp, bias=b15,
                                        scale=scaleC), reads=[bS, bK], writes=[bpt])
                                pts.append((pt, bpt))
                            cur = (kt, c0, pts)
                        else:
                            cur = None
                        if pend is not None:
                            pk, pc0, ppts = pend
                            for m in range(2):
                                ppt, pbpt = ppts[m]
                                Ob, bO = Os[2 * m]
                                Lb, bLb = Os[2 * m + 1]
                                sc.op("pe", lambda e, pk=pk, pc0=pc0, ppt=ppt, Ob=Ob: e.matmul(
                                    Ob[:, pc0:512], lhsT=v[:, pk, :], rhs=ppt[:, pc0:512], start=(pk == 0),
                                    stop=(pk == nkt - 1)), reads=[bv, pbpt], writes=[bO], sig=False)
                                sc.op("pe", lambda e, pk=pk, pc0=pc0, ppt=ppt, Lb=Lb: e.matmul(
                                    Lb[:, pc0:512], lhsT=ones[:], rhs=ppt[:, pc0:512], start=(pk == 0),
                                    stop=(pk == nkt - 1)), reads=[bK, pbpt], writes=[bLb])
                        pend = cur
                    cs = slice(j * 512, j * 512 + 512)
                    gt, bg = gch.next()
                    sc.dma(gt[:], GC[h, :, cs], reads=[dB["GC"]], writes=[bg])
                    (O1, bO1), (L1, bL1), (O2, bO2), (L2, bL2) = Os
                    r1, b1 = ep.next()
                    a1, ba1 = ep.next()
                    r2, b2 = ep.next()
                    a2, ba2 = ep.next()
                    sc.op("dve", lambda e: e.tensor_copy(out=r1[:], in_=L1[:, :]), reads=[bL1], writes=[b1])
                    sc.op("dve", lambda e: e.tensor_copy(out=a1[:], in_=O1[:, :]), reads=[bO1], writes=[ba1])
                    sc.op("dve", lambda e: e.tensor_copy(out=r2[:], in_=L2[:, :]), reads=[bL2], writes=[b2])
                    sc.op("dve", lambda e: e.tensor_copy(out=a2[:], in_=O2[:, :]), reads=[bO2], writes=[ba2])
                    sc.op("dve", lambda e: e.reciprocal(out=r1[:], in_=r1[:]), reads=[b1], writes=[b1])
                    sc.op("dve", lambda e: e.tensor_tensor(out=a1[:], in0=a1[:], in1=r1[:], op=ALU.mult),
                          reads=[ba1, b1], writes=[ba1])
                    sc.op("dve", lambda e: e.reciprocal(out=r2[:], in_=r2[:]), reads=[b2], writes=[b2])
                    sc.op("dve", lambda e: e.tensor_tensor(out=a2[:], in0=a2[:], in1=r2[:], op=ALU.mult),
                          reads=[ba2, b2], writes=[ba2])
                    o, bo = ep.next()
                    sc.op("dve", lambda e: e.scalar_tensor_tensor(out=o[:], in0=a2[:], scalar=neglam, in1=a1[:],
                                                                  op0=ALU.mult, op1=ALU.add),
                          reads=[ba1, ba2, bL], writes=[bo])
                    def partA(o=o, bo=bo):
                        sc.op("act", lambda e: e.activation(out=sq16[:], in_=o[:], func=AF.Square), reads=[bo],
                              writes=[bsq16])

                    def partB(box=[None]):
                        Sp, bS = Sb.next()
                        sc.op("pe", lambda e: e.matmul(Sp[:, :], lhsT=ones[:], rhs=sq16[:], start=True, stop=True),
                              reads=[bsq16, bK], writes=[bS])
                        return (Sp, bS)

                    def partC(SpbS, o=o, bo=bo, r1=r1, b1=b1, a1=a1, ba1=ba1, gt=gt, bg=bg, cs=cs, h=h):
                        Sp, bS = SpbS
                        rt, brt = ep.next()
                        sc.op("act", lambda e: e.activation(out=rt[:], in_=Sp[:, :], func=AF.Ln, scale=1.0 / 128.0,
                                                            bias=EPS), reads=[bS], writes=[brt])
                        sc.op("act", lambda e: e.activation(out=r1[:], in_=rt[:], func=AF.Exp, scale=-0.5),
                              reads=[brt], writes=[b1])
                        sc.op("dve", lambda e: e.scalar_tensor_tensor(out=a1[:], in0=o[:], scalar=gsub_s, in1=r1[:],
                                                                      op0=ALU.mult, op1=ALU.mult),
                              reads=[bo, b1, bL], writes=[ba1])
                        mo, bm = mx16.next()
                        sc.op("pool", lambda e: e.tensor_tensor(out=mo[:], in0=a1[:], in1=gt[:], op=ALU.mult),
                              reads=[ba1, bg], writes=[bm])
                        sc.dma(MIX[12 + h, :, cs], mo[:], reads=[bm], writes=[dB["MIX"]])
                    pendC.append([partA, partB, partC])
            flushC()
            sc.barrier()
        if stop_after in (("T", l), ("T0", l), ("TA", l), ("TB", l)):
            break

        with contextlib.ExitStack() as ps:
            def psb(name, shape, dt):
                return ps.enter_context(nc.sbuf_tensor(f"{name}_L{l}", list(shape), dt))

            def ppm(name, shape, dt=F32):
                return ps.enter_context(nc.psum_tensor(f"{name}_L{l}", list(shape), dt))

            wo = psb("wo", [128, 16, D], BF16)
            bwo = [Buf(f"wo{i}") for i in range(8)]
            bwo2 = [Buf(f"wo2_{i}") for i in range(8)]
            wos = Ring([psb(f"wos{i}", [128, 16, 256], F32) for i in range(2)])
            mxr = Ring([psb(f"mxt{i}", [128, 16, 512], BF16) for i in range(2)])
            xr = Ring([psb(f"xr{i}", [128, D], F32) for i in range(2)])
            xn = Ring([psb(f"xn{i}", [128, D], F32) for i in range(2)])
            yo = Ring([psb(f"yo{i}", [128, D], F32) for i in range(2)])
            fg = psb("fg", [128, D], F32)
            bfg = Buf("fg")
            junk = psb("junk", [128, D], BF16)
            bjunk = Buf("junk")
            ssq = Ring([psb(f"ossq{i}", [128, 1], F32) for i in range(2)])
            rstd = Ring([psb(f"orstd{i}", [128, 1], F32) for i in range(2)])
            op_ = Ring([ppm(f"op{i}", [128, 512]) for i in range(4)])
            if last:
                sc.dma(fg[:], final_g.rearrange("(o d) -> o d", o=1).partition_broadcast(128), writes=[bfg])
            wo_l = w_out[l]
            pre = []
            for cg in range(2):
                t, b = wos.next()
                sc.dma(t[:], wo_l[:, cg * 256:cg * 256 + 256].rearrange("(k p) n -> p k n", p=128), writes=[b])
                pre.append((t, b))
            for cg in range(8):
                t, b = pre[cg % 2]
                sc.op("act", lambda e, t=t, cg=cg: e.copy(out=wo[:, 0:8, cg * 256:cg * 256 + 256], in_=t[:, 0:8, :]),
                      reads=[b], writes=[bwo[cg]])
                sc.op("dve", lambda e, t=t, cg=cg: e.tensor_copy(out=wo[:, 8:16, cg * 256:cg * 256 + 256],
                                                                 in_=t[:, 8:16, :]),
                      reads=[b], writes=[bwo2[cg]])
                if cg + 2 < 8:
                    t2, b2 = wos.next()
                    sc.dma(t2[:], wo_l[:, (cg + 2) * 256:(cg + 2) * 256 + 256].rearrange("(k p) n -> p k n", p=128),
                           writes=[b2])
                    pre[cg % 2] = (t2, b2)
            ydst = y_out if last else X1
            ydname = "Y" if last else "X1"
            if split:
                xsel = psb("xsel", [128, D], F32)
                bxsel = Buf("xsel")
            for c4 in range(NOWN // 4):
                mt, bmt = mxr.next()
                sc.dma(mt[:], MIX[:, :, c4 * 512:c4 * 512 + 512].rearrange("f p t -> p f t"), reads=[dB["MIX"]],
                       writes=[bmt])
                for t4 in range(4):
                    tt = c4 * 4 + t4
                    if split:
                        xa, bxa = xr.next()
                        xb, bxb = xr.next()
                        sc.dma(xa[:], xsrc[2 * tt * 128:2 * tt * 128 + 128, :], reads=[xsrc_b], writes=[bxa])
                        sc.dma(xb[:], xsrc[(2 * tt + 1) * 128:(2 * tt + 1) * 128 + 128, :], reads=[xsrc_b],
                               writes=[bxb])
                        sc.op("act", lambda e, xa=xa: e.activation(out=xsel[:], in_=xa[:], func=AF.Copy, scale=ra),
                              reads=[bxa, bK], writes=[bxsel])
                        sc.op("dve", lambda e, xb=xb: e.scalar_tensor_tensor(out=xsel[:], in0=xb[:], scalar=rb_,
                                                                             in1=xsel[:], op0=ALU.mult, op1=ALU.add),
                              reads=[bxb, bxsel, bK], writes=[bxsel])
                        xt, bx = xsel, bxsel
                    else:
                        xt, bx = xr.next()
                        sc.dma(xt[:], xsrc[tt * 128:tt * 128 + 128, :], reads=[xsrc_b], writes=[bx])
                    xo, bxo = xn.next()
                    for cg in range(4):
                        pb_, bpb = op_.next()
                        for m in range(16):
                            sc.op("pe", lambda e, m=m, pb_=pb_, cg=cg, t4=t4, mt=mt: e.matmul(
                                pb_[:, :], lhsT=mt[:, m, t4 * 128:t4 * 128 + 128],
                                rhs=wo[:, m, cg * 512:cg * 512 + 512], start=(m == 0), stop=(m == 15)),
                                reads=[bmt, bwo[2 * cg], bwo[2 * cg + 1], bwo2[2 * cg], bwo2[2 * cg + 1]],
                                writes=[bpb], sig=(m == 15))
                        sc.op("dve", lambda e, pb_=pb_, cg=cg, xo=xo, xt=xt: e.tensor_tensor(
                            out=xo[:, cg * 512:cg * 512 + 512], in0=pb_[:, :], in1=xt[:, cg * 512:cg * 512 + 512],
                            op=ALU.add), reads=[bpb, bx], writes=[bxo])
                    if not last:
                        sc.dma(X1[tt * 128:tt * 128 + 128, :], xo[:], reads=[bxo], writes=[dB["X1"]])
                    else:
                        sq_t, bsq = ssq.next()
                        rs_t, brs = rstd.next()
                        sc.op("act", lambda e, xo=xo, sq_t=sq_t: e.activation(out=junk[:], in_=xo[:], func=AF.Square,
                                                                               accum_out=sq_t[:]),
                              reads=[bxo], writes=[bjunk, bsq])
                        sc.op("act", lambda e, sq_t=sq_t: e.activation(out=sq_t[:], in_=sq_t[:], func=AF.Ln,
                                                                       scale=1.0 / D, bias=EPS),
                              reads=[bsq], writes=[bsq])
                        sc.op("act", lambda e, sq_t=sq_t, rs_t=rs_t: e.activation(out=rs_t[:], in_=sq_t[:],
                                                                                  func=AF.Exp, scale=-0.5),
                              reads=[bsq], writes=[brs])
                        yt, byt = yo.next()
                        sc.op("dve", lambda e, xo=xo, rs_t=rs_t, yt=yt: e.scalar_tensor_tensor(
                            out=yt[:], in0=xo[:], scalar=rs_t[:, 0:1], in1=fg[:], op0=ALU.mult, op1=ALU.mult),
                            reads=[bxo, brs, bfg], writes=[byt])
                        sc.dma(y_out[tt * 128:tt * 128 + 128, :], yt[:], reads=[byt], writes=[dB["Y"]])
            sc.barrier()
        if stop_after == ("O", l):
            break

    sc.finish()
    stack.close()
    return nc, sc


def _t5_bucket(rel):
    nb = 16
    me = 8
    ret = (rel > 0).astype(np.int32) * nb
    n = np.abs(rel)
    nf = np.maximum(n, 1).astype(np.float32)
    large = me + (np.log(nf / me) / math.log(128 / me) * (nb - me)).astype(np.int32)
    large = np.minimum(large, nb - 1)
    return ret + np.where(n < me, n, large)


def host_constants(r=0):
    c = {}
    c["c_ident"] = np.eye(128, dtype=np.float32).astype(ml_dtypes.bfloat16)
    c["c_identf"] = np.eye(128, dtype=np.float32)
    c["c_ones"] = np.ones((128, 128), np.float32).astype(ml_dtypes.bfloat16)
    half = 32
    inv = (10000.0 ** (-np.arange(half, dtype=np.float32) / half)).astype(np.float32)
    ang = (np.arange(S, dtype=np.float32)[:, None] * inv[None, :]).astype(np.float32)
    cos = np.cos(ang).T.astype(np.float32)
    sin = np.sin(ang).T.astype(np.float32)
    cos2 = np.ascontiguousarray(np.concatenate([cos, cos], axis=0))
    sin2 = np.ascontiguousarray(np.concatenate([-sin, sin], axis=0))
    c["c_cos2"] = cos2
    c["c_sin2"] = sin2
    own_cols = np.concatenate([np.arange((2 * i + r) * 128, (2 * i + r + 1) * 128) for i in range(NT // 2)])
    c["c_cos_own"] = np.ascontiguousarray(cos2[:, own_cols])
    c["c_sin_own"] = np.ascontiguousarray(sin2[:, own_cols])
    kl = np.arange(128)[:, None]
    ql = np.arange(128)[None, :]
    mA = np.where(kl <= ql, 0.0, NEG).astype(np.float32)
    mB = np.where((kl // 64) <= (ql // 64), 0.0, NEG).astype(np.float32)
    c["c_maskA"] = mA
    c["c_maskB"] = mB
    closed = np.full((128, 128), NEG, np.float32)
    opened = np.zeros((128, 128), np.float32)
    c["c_mA2"] = np.ascontiguousarray(np.concatenate([mA, closed] if r == 0 else [opened, mA], axis=1))
    c["c_mB2"] = np.ascontiguousarray(np.concatenate([mB, closed] if r == 0 else [opened, mB], axis=1))
    bk = np.zeros((128, 32, 256), np.float32)
    for off, d in ((0, -128), (1, 0)):
        bu = _t5_bucket(kl - ql + d)
        for b in range(32):
            bk[:, b, off * 128:(off + 1) * 128] = (bu == b)
    c["c_bkt"] = bk.astype(ml_dtypes.bfloat16)
    c["c_add2"] = np.ascontiguousarray(np.concatenate([opened, mB], axis=1))
    bk3 = np.zeros((128, 32, 384), np.float32)
    add3 = np.zeros((128, 384), np.float32)
    for idx in range(3):
        delta = idx - 1 - r
        sl = slice(idx * 128, (idx + 1) * 128)
        if delta > 0:
            add3[:, sl] = NEG
            continue
        bu = _t5_bucket(kl - ql + delta * 128)
        for b in range(32):
            bk3[:, b, sl] = (bu == b)
        if delta == 0:
            add3[:, sl] = mB
    c["c_bkt3"] = bk3.astype(ml_dtypes.bfloat16)
    c["c_add3"] = add3
    sel = np.zeros((6, 768), np.float32)
    for h in range(6):
        sel[h, h * 128:(h + 1) * 128] = 1.0
    c["c_sel"] = sel
    role = np.zeros((128, 2), np.float32)
    role[:, r] = 1.0
    c["c_role"] = role
    return c


_CACHE = {}


def kernel(x, norm_g, w_in, b_forget, mla_q_norm_g, w_uq, mla_kv_norm_g, w_ukv,
           lambda_q1, lambda_k1, lambda_q2, lambda_k2, diff_subln_g, rel_bias,
           w_out, final_norm_g):
    f = lambda a: np.ascontiguousarray(np.asarray(a, dtype=np.float32))
    if "nc" not in _CACHE:
        _CACHE["nc"] = build_program()[0]
    nc = _CACHE["nc"]
    consts = [host_constants(0), host_constants(1)]
    shared = dict(norm_g=f(norm_g), w_in=f(w_in), b_forget=f(b_forget), mla_q_norm_g=f(mla_q_norm_g),
                  w_uq=f(w_uq), mla_kv_norm_g=f(mla_kv_norm_g), w_ukv=f(w_ukv), lambda_q1=f(lambda_q1),
                  lambda_k1=f(lambda_k1), lambda_q2=f(lambda_q2), lambda_k2=f(lambda_k2),
                  diff_subln_g=f(diff_subln_g), rel_bias=f(rel_bias), w_out=f(w_out),
                  final_norm_g=f(final_norm_g))
    xs = f(x)
    in_maps = []
    for c in range(8):
        m = dict(shared)
        m.update(consts[c % 2])
        m["x"] = xs[c // 2]
        in_maps.append(m)
    res = run_bass_kernel_spmd(nc, in_maps, core_ids=list(range(8)))
    out = np.empty((4, S, D), np.float32)
    for c in range(8):
        b, r = c // 2, c % 2
        y = np.asarray(res.results[c]["y"], dtype=np.float32).reshape(NT // 2, 128, D)
        out.reshape(4, NT // 2, 2, 128, D)[b, :, r] = y
    return out
```
